# Optimizing a Trainium2 kernel written in Bass

```python
import jax, jax.numpy as jnp
from jax import lax
import numpy as np

D_MODEL = 2048
BATCH = 4
SEQ = 4096
DEPTH = 1

PLE_DIM = 256
ATTN_HEADS = 8
ATTN_HEAD_DIM = 128
ATTN_BLOCK = 128
MLSTM_HEADS = 4
MLSTM_QK_DIM = 128
MLSTM_V_DIM = 256
MLSTM_CHUNK = 64
CONV_WIDTH = 4
ATTN_WIDTH = ATTN_HEADS * ATTN_HEAD_DIM
MLSTM_QK_WIDTH = MLSTM_HEADS * MLSTM_QK_DIM
MLSTM_WIDTH = MLSTM_HEADS * MLSTM_V_DIM
MIX_WIDTH = ATTN_WIDTH + MLSTM_WIDTH
D_FF = -(-8 * D_MODEL // (3 * 256)) * 256
IN_WIDTHS = (ATTN_WIDTH, ATTN_WIDTH, ATTN_WIDTH, ATTN_HEADS,
             MLSTM_QK_WIDTH, MLSTM_QK_WIDTH, MLSTM_WIDTH, MLSTM_HEADS, MLSTM_HEADS, MLSTM_WIDTH)
IN_COLS = sum(IN_WIDTHS)
EPS = 1e-6

kernel_name = 'fox_mlstm_parallel_hybrid_block'


def rms_norm(u, w):
    uf = u.astype(jnp.float32)
    y = uf * lax.rsqrt(jnp.mean(uf * uf, axis=-1, keepdims=True) + EPS)
    return y.astype(u.dtype) * w


def causal_dwconv(u, w, b):
    K, C = w.shape
    out = lax.conv_general_dilated(u, w[:, None, :], window_strides=(1,), padding=[(K - 1, 0)],
                                   dimension_numbers=('NWC', 'WIO', 'NWC'), feature_group_count=C)
    return out + b


def forgetting_attention(q, k, v, logf):
    B, S, H, d = q.shape
    q = q.transpose(0, 2, 1, 3)
    k = k.transpose(0, 2, 1, 3)
    v = v.transpose(0, 2, 1, 3)
    c = jnp.cumsum(logf, axis=1).transpose(0, 2, 1)
    kpos = jnp.arange(S)
    scale = d ** -0.5

    def block(bi):
        start = bi * ATTN_BLOCK
        qb = lax.dynamic_slice_in_dim(q, start, ATTN_BLOCK, axis=2)
        cb = lax.dynamic_slice_in_dim(c, start, ATTN_BLOCK, axis=2)
        s = jnp.einsum('bhqd,bhkd->bhqk', qb, k).astype(jnp.float32) * scale
        s = s + cb[..., :, None] - c[..., None, :]
        qpos = start + jnp.arange(ATTN_BLOCK)
        s = jnp.where(kpos[None, :] <= qpos[:, None], s, -jnp.inf)
        pr = jax.nn.softmax(s, axis=-1).astype(v.dtype)
        return jnp.einsum('bhqk,bhkd->bhqd', pr, v)

    o = lax.map(block, jnp.arange(S // ATTN_BLOCK))
    return o.transpose(1, 0, 3, 2, 4).reshape(B, S, H * d)


def mlstm_chunkwise(q, k, v, i_pre, logf):
    B, H, S, dk = q.shape
    dv = v.shape[-1]
    L = MLSTM_CHUNK
    NC = S // L
    qc = jnp.moveaxis(q.reshape(B, H, NC, L, dk), 2, 0)
    kc = jnp.moveaxis(k.reshape(B, H, NC, L, dk), 2, 0)
    vc = jnp.moveaxis(v.reshape(B, H, NC, L, dv), 2, 0)
    ic = jnp.moveaxis(i_pre.reshape(B, H, NC, L), 2, 0)
    fc = jnp.moveaxis(logf.reshape(B, H, NC, L), 2, 0)
    causal = jnp.tril(jnp.ones((L, L), dtype=bool))

    def step(carry, xs):
        C, n, m = carry
        qt, kt, vt, it, ft = xs
        b = jnp.cumsum(ft, axis=-1)
        D = b[..., :, None] - b[..., None, :] + it[..., None, :]
        D = jnp.where(causal, D, -jnp.inf)
        inter = b + m[..., None]
        m_t = jnp.maximum(inter, jnp.max(D, axis=-1))
        w_intra = jnp.exp(D - m_t[..., None])
        w_inter = jnp.exp(inter - m_t)
        s_qk = jnp.einsum('bhtd,bhsd->bhts', qt, kt) * w_intra
        num = jnp.einsum('bhts,bhsv->bhtv', s_qk, vt) + w_inter[..., None] * jnp.einsum('bhvd,bhtd->bhtv', C, qt)
        den = jnp.sum(s_qk, axis=-1) + w_inter * jnp.einsum('bhd,bhtd->bht', n, qt)
        h = num / jnp.maximum(jnp.abs(den), jnp.exp(-m_t))[..., None]
        bL = b[..., -1]
        g = bL[..., None] - b + it
        m_new = jnp.maximum(bL + m, jnp.max(g, axis=-1))
        ws = jnp.exp(g - m_new[..., None])
        wc = jnp.exp(bL + m - m_new)
        C_new = wc[..., None, None] * C + jnp.einsum('bhs,bhsv,bhsd->bhvd', ws, vt, kt)
        n_new = wc[..., None] * n + jnp.einsum('bhs,bhsd->bhd', ws, kt)
        return (C_new, n_new, m_new), h

    init = (jnp.zeros((B, H, dv, dk), jnp.float32), jnp.zeros((B, H, dk), jnp.float32), jnp.zeros((B, H), jnp.float32))
    _, hs = lax.scan(step, init, (qc, kc, vc, ic, fc))
    return jnp.moveaxis(hs, 0, 2).reshape(B, H, S, dv)


def setup_inputs(seed: int = 0) -> dict:
    key = jax.random.key(seed)
    ks = jax.random.split(key, 24)
    f32 = jnp.float32

    def nrm(k, shape, scale):
        return jax.random.normal(k, shape, f32) * scale

    def gain(k, shape):
        return 1.0 + 0.01 * jax.random.normal(k, shape, f32)

    return {
        'x': nrm(ks[0], (BATCH, SEQ, D_MODEL), 1.0),
        'p': nrm(ks[1], (DEPTH, BATCH, SEQ, PLE_DIM), 1.0),
        'w_norm_mix': gain(ks[2], (DEPTH, D_MODEL)),
        'w_in': nrm(ks[3], (DEPTH, D_MODEL, IN_COLS), D_MODEL ** -0.5),
        'fox_f_bias': jnp.linspace(1.0, 5.0, ATTN_HEADS, dtype=f32)[None, :] + nrm(ks[4], (DEPTH, ATTN_HEADS), 0.1),
        'q_norm_w': gain(ks[5], (DEPTH, ATTN_HEAD_DIM)),
        'k_norm_w': gain(ks[6], (DEPTH, ATTN_HEAD_DIM)),
        'mlstm_conv_w': nrm(ks[7], (DEPTH, CONV_WIDTH, 2 * MLSTM_QK_WIDTH), CONV_WIDTH ** -0.5),
        'mlstm_conv_b': nrm(ks[8], (DEPTH, 2 * MLSTM_QK_WIDTH), 0.01),
        'mlstm_i_bias': nrm(ks[9], (DEPTH, MLSTM_HEADS), 0.1),
        'mlstm_f_bias': jnp.linspace(3.0, 6.0, MLSTM_HEADS, dtype=f32)[None, :] + nrm(ks[10], (DEPTH, MLSTM_HEADS), 0.1),
        'mlstm_out_norm_w': gain(ks[11], (DEPTH, MLSTM_WIDTH)),
        'w_out': nrm(ks[12], (DEPTH, MIX_WIDTH, D_MODEL), MIX_WIDTH ** -0.5),
        'w_norm_ffn': gain(ks[13], (DEPTH, D_MODEL)),
        'w_ffn_gate': nrm(ks[14], (DEPTH, D_MODEL, D_FF), D_MODEL ** -0.5),
        'w_ffn_up': nrm(ks[15], (DEPTH, D_MODEL, D_FF), D_MODEL ** -0.5),
        'w_ffn_down': nrm(ks[16], (DEPTH, D_FF, D_MODEL), D_FF ** -0.5),
        'w_norm_ple': gain(ks[17], (DEPTH, D_MODEL)),
        'w_ple_gate': nrm(ks[18], (DEPTH, D_MODEL, D_MODEL), D_MODEL ** -0.5),
        'w_ple_proj': nrm(ks[19], (DEPTH, PLE_DIM, D_MODEL), PLE_DIM ** -0.5),
        'w_ple_post_norm': gain(ks[20], (DEPTH, D_MODEL)),
    }


def reference(x, p, w_norm_mix, w_in, fox_f_bias, q_norm_w, k_norm_w, mlstm_conv_w, mlstm_conv_b,
              mlstm_i_bias, mlstm_f_bias, mlstm_out_norm_w, w_out, w_norm_ffn, w_ffn_gate, w_ffn_up,
              w_ffn_down, w_norm_ple, w_ple_gate, w_ple_proj, w_ple_post_norm):
    B, S, _ = x.shape
    f32 = jnp.float32
    split_points = np.cumsum(IN_WIDTHS)[:-1].tolist()
    for i in range(DEPTH):
        h = rms_norm(x, w_norm_mix[i])
        proj = h @ w_in[i]
        aq, ak, av, af, mq, mk, mv, mi, mf, mo = jnp.split(proj, split_points, axis=-1)

        aq = rms_norm(aq.reshape(B, S, ATTN_HEADS, ATTN_HEAD_DIM), q_norm_w[i])
        ak = rms_norm(ak.reshape(B, S, ATTN_HEADS, ATTN_HEAD_DIM), k_norm_w[i])
        av = av.reshape(B, S, ATTN_HEADS, ATTN_HEAD_DIM)
        a_logf = jax.nn.log_sigmoid(af.astype(f32) + fox_f_bias[i].astype(f32))
        attn_out = forgetting_attention(aq, ak, av, a_logf)

        mqk = jax.nn.silu(causal_dwconv(jnp.concatenate([mq, mk], axis=-1), mlstm_conv_w[i], mlstm_conv_b[i]))
        mq, mk = jnp.split(mqk, 2, axis=-1)
        mq = mq.reshape(B, S, MLSTM_HEADS, MLSTM_QK_DIM).transpose(0, 2, 1, 3).astype(f32) * (MLSTM_QK_DIM ** -0.5)
        mk = mk.reshape(B, S, MLSTM_HEADS, MLSTM_QK_DIM).transpose(0, 2, 1, 3).astype(f32)
        mv = mv.reshape(B, S, MLSTM_HEADS, MLSTM_V_DIM).transpose(0, 2, 1, 3).astype(f32)
        m_i = (mi.astype(f32) + mlstm_i_bias[i].astype(f32)).transpose(0, 2, 1)
        m_logf = jax.nn.log_sigmoid(mf.astype(f32) + mlstm_f_bias[i].astype(f32)).transpose(0, 2, 1)
        ht = mlstm_chunkwise(mq, mk, mv, m_i, m_logf)
        ht = rms_norm(ht.transpose(0, 2, 1, 3), jnp.ones((), f32)).reshape(B, S, MLSTM_WIDTH)
        mlstm_out = (ht.astype(x.dtype) * mlstm_out_norm_w[i]) * jax.nn.sigmoid(mo)

        x = x + jnp.concatenate([attn_out, mlstm_out], axis=-1) @ w_out[i]

        h2 = rms_norm(x, w_norm_ffn[i])
        x = x + (jax.nn.silu(h2 @ w_ffn_gate[i]) * (h2 @ w_ffn_up[i])) @ w_ffn_down[i]

        gate = jax.nn.sigmoid(rms_norm(x, w_norm_ple[i]) @ w_ple_gate[i])
        e = rms_norm(p[i] @ w_ple_proj[i], w_ple_post_norm[i])
        x = x + gate * e
    return x
```

```python
import numpy as np
import concourse.bass as bass
import concourse.mybir as mybir
from concourse.bass_utils import run_bass_kernel_spmd

F32 = mybir.dt.float32
BF16 = mybir.dt.bfloat16
AF = mybir.ActivationFunctionType
ALU = mybir.AluOpType
AX = mybir.AxisListType

S = 4096
D = 2048
DFF = 5632
NKC = 16
W1 = 3080
EPS = 1e-6
NEG = -30000.0
DEBUG = None


class Buf:
    __slots__ = ("w", "r", "name")

    def __init__(self, name=""):
        self.w = None
        self.r = []
        self.name = name


class Op:
    __slots__ = ("eng", "idx", "fn", "deps", "dma", "dsem", "dval", "prev_dma", "cc")


NDSEM = 12
COMPUTE = ("pe", "act", "dve", "pool")


class Sched:
    def __init__(self, nc, sems):
        self.nc = nc
        self.sems = sems
        self.q = {e: [] for e in ("pe", "act", "dve", "pool", "sp")}
        self.count = {e: 0 for e in COMPUTE}
        self.ndma = {"sp": 0, "pool": 0, "act": 0}
        self.dma_ops = {"sp": [], "pool": [], "act": []}
        self.waited = {e: {} for e in self.q}
        self.cc_ops = []

    def add(self, eng, fn, reads=(), writes=(), dma=False):
        op = Op()
        op.eng = eng
        op.fn = fn
        op.dma = dma
        op.cc = False
        deps = []
        for b in reads:
            if b.w is not None:
                deps.append(b.w)
        for b in writes:
            if b.w is not None:
                deps.append(b.w)
            deps.extend(b.r)
        for b in reads:
            b.r.append(op)
        for b in writes:
            b.w = op
            b.r = []
        op.deps = [d for d in deps if d is not op]
        if dma:
            n = self.ndma[eng]
            op.dsem = self.sems[("d", eng, n % NDSEM)]
            op.dval = 16 * (n // NDSEM + 1)
            lag = 1 if eng == "pool" else NDSEM
            op.prev_dma = self.dma_ops[eng][n - lag] if n >= lag else None
            self.ndma[eng] = n + 1
            self.dma_ops[eng].append(op)
            op.idx = None
        else:
            self.count[eng] += 1
            op.idx = self.count[eng]
        self.q[eng].append(op)
        return op

    def _wait_for(self, eng, handle, dep):
        if dep.dma:
            sem, val = dep.dsem, dep.dval
        else:
            if dep.eng == "pe" and eng == "pe":
                return
            sem, val = self.sems[dep.eng], dep.idx
        key = id(sem)
        if self.waited[eng].get(key, 0) >= val:
            return
        self.waited[eng][key] = val
        handle.wait_ge(sem, val)

    def _emit_one(self, eng, h, final):
        for op in self.q[eng]:
            for d in op.deps:
                self._wait_for(eng, h, d)
            if op.dma and op.prev_dma is not None:
                self._wait_for(eng, h, op.prev_dma)
            ins = op.fn(h)
            if op.cc:
                ins.then_inc(op.dsem)
            elif op.dma:
                ins.then_inc(op.dsem, 16)
            else:
                ins.then_inc(self.sems[eng], 1)
        self.q[eng] = []
        for e2 in COMPUTE:
            if e2 == eng or final[e2] == 0:
                continue
            sem = self.sems[e2]
            if self.waited[eng].get(id(sem), 0) < final[e2]:
                self.waited[eng][id(sem)] = final[e2]
                h.wait_ge(sem, final[e2])
        for qn, lst in self.dma_ops.items():
            for op in lst[-NDSEM:]:
                self._wait_for(eng, h, op)

    def run_block(self):
        final = {e: self.count[e] for e in COMPUTE}
        with self.nc.Block() as block:
            @block.tensor
            def _(e):
                self._emit_one("pe", e, final)

            @block.scalar
            def _(e):
                self._emit_one("act", e, final)

            @block.vector
            def _(e):
                self._emit_one("dve", e, final)

            @block.gpsimd
            def _(e):
                self._emit_one("pool", e, final)

            @block.sync
            def _(e):
                self._emit_one("sp", e, final)


_REG = {}


def _fillreg(e, val):
    if val not in _REG:
        _REG[val] = e.to_reg(val)
    return _REG[val]


def build_nc(debug=None):
    _REG.clear()
    nc = bass.Bass("TRN2", target_bir_lowering=False)

    lite = debug is not None and debug.get("lite")

    def din(name, shape, dt=F32):
        if lite and name in ("wout", "wg", "wu", "wd", "wpg", "wpp"):
            shape = [128, 128]
        if lite and name == "xb":
            shape = [debug.get("ntt", 8) * 512, D]
        if lite and name in ("xo", "po"):
            shape = [128, shape[1]]
        return nc.dram_tensor(name, list(shape), dt, kind="ExternalInput").ap()

    xb = din("xb", [S, D])
    xo = din("xo", [2048, D])
    po = din("po", [2048, 256])
    w1 = din("w1", [D, W1])
    wnmix = din("wnmix", [128, NKC])
    bias8 = din("bias8", [8, 1])
    qkg = din("qkg", [128, 2])
    convw = din("convw", [128, 4, 4])
    convb = din("convb", [128, 4])
    onw = din("onw", [128, 4])
    wout = din("wout", [D, D])
    wnffn = din("wnffn", [128, NKC])
    wg = din("wg", [D, DFF])
    wu = din("wu", [D, DFF])
    wd = din("wd", [DFF, D])
    wnple = din("wnple", [128, NKC])
    wpg = din("wpg", [D, D])
    wpp = din("wpp", [256, D])
    wpost = din("wpost", [1, D])
    y = nc.dram_tensor("y", [128 if lite else 2048, D], F32, kind="ExternalOutput").ap()

    dbg = {}

    def dscr(name, shape, dt):
        if debug is not None and (name in debug or (name.startswith("bounce") and "bounce" in debug)):
            t = nc.dram_tensor(name, list(shape), dt, kind="ExternalOutput")
        else:
            t = nc.dram_tensor(name, list(shape), dt)
        return t

    qk_scr = dscr("qk_scr", [8, 128, S], BF16).ap()
    mqk_scr = dscr("mqk_scr", [4, 128, S], F32).ap()
    mo_scr = dscr("mo_scr", [4, 128, S], BF16).ap()
    v_scr = dscr("v_scr", [S, 1024], BF16).ap()
    g_scr = dscr("g_scr", [16, S], F32).ap()
    m_scr = dscr("m_scr", [1, 16], F32).ap()
    dbg_bounce = debug is not None and "bounce" in debug
    bounce_t = [dscr("bounce%d" % c, [128, S], BF16) for c in range(8)]
    gathered_t = [nc.dram_tensor("gathered%d" % c, [256, S], BF16) for c in range(8)]
    bounce = [t.ap() for t in bounce_t]

    names = {}
    with (
        nc.semaphore("s_pe") as s_pe, nc.semaphore("s_act") as s_act,
        nc.semaphore("s_dve") as s_dve, nc.semaphore("s_pool") as s_pool,
        nc.semaphore("s_cc") as s_cc,
    ):
        import contextlib
        es = contextlib.ExitStack()
        with es:
            sems = {"pe": s_pe, "act": s_act, "dve": s_dve, "pool": s_pool}
            for qn in ("sp", "pool", "act"):
                for j in range(NDSEM):
                    sems[("d", qn, j)] = es.enter_context(nc.semaphore(f"d_{qn}_{j}"))
            sc = Sched(nc, sems)

            run_block = sc.run_block

            ident_bf = es.enter_context(nc.sbuf_tensor("ident_bf", [128, 128], BF16))
            ident_f = es.enter_context(nc.sbuf_tensor("ident_f", [128, 128], F32))
            ones_bf = es.enter_context(nc.sbuf_tensor("ones_bf", [128, 128], BF16))
            b_ident = Buf()
            b_ones = Buf()

            for t_ in (ident_bf, ident_f):
                sc.add("pool", (lambda t_=t_: lambda e: e.memset(t_[:], 1.0))(), writes=[b_ident])
                sc.add("pool", (lambda t_=t_: lambda e: e.affine_select(out=t_[:], in_=t_[:], pattern=[[-1, 128]], compare_op=ALU.is_equal,
                                                                       fill=0.0, base=0, channel_multiplier=1))(), writes=[b_ident])
            sc.add("pool", lambda e: e.memset(ones_bf[:], 1.0), writes=[b_ones])

            ptr = es.enter_context(nc.psum_tensor("ptr", [128, 2, 1024], BF16))
            pac = es.enter_context(nc.psum_tensor("pac", [128, 6, 512], F32))
            b_ptr = [Buf(), Buf()]
            b_pac = [Buf() for _ in range(6)]

            gs = contextlib.ExitStack()
            gall = gs.enter_context(nc.sbuf_tensor("gall", [8, S], F32))
            b_gathered = [Buf() for _ in range(8)]
            b_gall = Buf()
            sc.add("pool", lambda e: e.memset(gall[:], 0.0), writes=[b_gall])

            wb = {}
            b_wb = {}
            conv_list = []
            for name, src, rows, cols in (("wout", wout, D, D), ("wg", wg, D, DFF), ("wu", wu, D, DFF), ("wd", wd, DFF, D),
                                          ("wpg", wpg, D, D), ("wpp", wpp, 256, D)):
                if lite:
                    continue
                wb[name] = nc.dram_tensor(name + "_b", [rows, cols], BF16).ap()
                b_wb[name] = Buf()
                sv = src.rearrange("(c p) n -> p c n", p=128)
                dv = wb[name].rearrange("(c p) n -> p c n", p=128)
                for c0 in range(0, cols, 512):
                    conv_list.append((name, sv[:, :, c0:c0 + 512], dv[:, :, c0:c0 + 512]))

            def emit_conv(k):
                for (name, s_ap, d_ap) in conv_list[:k]:
                    sc.add("pool", (lambda s_ap=s_ap, d_ap=d_ap: lambda e: e.dma_start(out=d_ap, in_=s_ap))(), reads=[b_wb[name]], dma=True)
                del conv_list[:k]
            only4 = debug is not None and debug.get("only4")
            ph = contextlib.ExitStack()
            with ph:
              if not only4:
                T = lambda name, shape, dt: ph.enter_context(nc.sbuf_tensor(name, shape, dt))
                w1s = T("w1s", [128, NKC, W1], BF16)
                xt = T("xt", [128, 2, D], F32)
                xn = T("xn", [128, 2, D], BF16)
                sqj = T("sqj", [128, D], BF16)
                hT2 = T("hT", [128, 2, NKC, 512], BF16)
                stat = T("stat", [128, 8], F32)
                gmix = T("gmix", [128, NKC], F32)
                qkgs = T("qkgs", [128, 2], F32)
                qraw = T("qraw", [128, 2, 512], F32)
                sq = T("sq", [128, 2, 512], BF16)
                rs = T("rs", [128, 2, 512], F32)
                rsc = T("rsc", [128, 2, 512], F32)
                qo = T("qo", [128, 2, 512], BF16)
                mqo = T("mqo", [128, 2, 512], F32)
                moo = T("moo", [128, 2, 512], BF16)
                vo = T("vo", [128, 2, 1024], BF16)
                epsb = T("epsb", [128, 1], F32)
                b_small = Buf()
                sc.add("pool", lambda e: e.memset(epsb[:], EPS), writes=[b_small])
                sc.add("sp", lambda e: e.dma_start(out=gmix[:], in_=wnmix), writes=[b_small], dma=True)
                sc.add("sp", lambda e: e.dma_start(out=qkgs[:], in_=qkg), writes=[b_small], dma=True)
                w1v = w1.rearrange("(c p) n -> p c n", p=128)
                b_w1p = [Buf() for _ in range(6)]
                b_w1 = [[Buf(), Buf()] for _ in range(4)]
                for pi in range(6 if not (debug or {}).get("skipw1") else 0):
                    sc.add("pool", (lambda pi=pi: lambda e: e.dma_start(out=w1s[:, :, pi * 512:(pi + 1) * 512], in_=w1v[:, :, pi * 512:(pi + 1) * 512]))(),
                           writes=[b_w1p[pi]], dma=True)
                gst = T("gst", [128, NKC, 8], F32)
                sc.add("sp", lambda e: e.dma_start(out=gst[:], in_=w1v[:, :, 3072:W1]), writes=[b_w1[3][0]], dma=True)
                sc.add("dve", lambda e: e.tensor_copy(out=w1s[:, :, 3072:W1], in_=gst[:]), reads=[b_w1[3][0]], writes=[b_w1[3][1]])
                b_xt = [Buf(), Buf()]
                b_xn = [Buf(), Buf()]
                b_sqj = Buf()
                b_hT = [Buf(), Buf()]
                b_stat = [Buf(), Buf()]
                b_qraw = [Buf(), Buf()]
                b_sq = [Buf(), Buf()]
                b_rs = [Buf(), Buf()]
                b_qo = [Buf(), Buf()]
                b_mqo = [Buf(), Buf()]
                b_moo = [Buf(), Buf()]
                b_vo = [Buf(), Buf()]
                pacn = [0]

                def next_pac():
                    i = pacn[0] % 6
                    pacn[0] += 1
                    return i

                xbv = xb.rearrange("(t p) d -> t p d", p=128)
                sub_global = [0]
                chunkctr = [0]
                NTT = 8 if debug is None else debug.get('ntt', 8)

                NSUBT = 4 * NTT
                stage_done = {"L": -1, "C": -1, "X": -1}

                def st_L(k):
                    r = k % 2
                    sc.add("sp", lambda e: e.dma_start(out=xt[:, r, :], in_=xbv[k]), writes=[b_xt[r]], dma=True)

                def st_C(k):
                    r = k % 2
                    sc.add("act", lambda e: e.activation(out=sqj[:], in_=xt[:, r, :], func=AF.Square, accum_out=stat[:, 4 * r:4 * r + 1]),
                           reads=[b_xt[r]], writes=[b_sqj, b_stat[r]])
                    sc.add("act", lambda e: e.activation(out=stat[:, 4 * r + 1:4 * r + 2], in_=stat[:, 4 * r:4 * r + 1], func=AF.Sqrt, bias=epsb[:], scale=1.0 / D),
                           reads=[b_small], writes=[b_stat[r]])
                    sc.add("dve", lambda e: e.reciprocal(out=stat[:, 4 * r + 2:4 * r + 3], in_=stat[:, 4 * r + 1:4 * r + 2]), writes=[b_stat[r]])
                    sc.add("dve", lambda e: e.tensor_scalar(out=xn[:, r, :], in0=xt[:, r, :], scalar1=stat[:, 4 * r + 2:4 * r + 3], scalar2=None, op0=ALU.mult),
                           reads=[b_xt[r], b_stat[r]], writes=[b_xn[r]])

                def st_X(k):
                    r = k % 2
                    tt_, s_ = divmod(k, 4)
                    hb = tt_ % 2
                    hT = hT2[:, hb]
                    for half in range(2):
                        def tr(e, half=half):
                            ins = None
                            for j in range(8):
                                kc = half * 8 + j
                                ins = e.transpose(out=ptr[:, half, j * 128:(j + 1) * 128], in_=xn[:, r, kc * 128:(kc + 1) * 128], identity=ident_bf[:])
                            return ins
                        sc.add("pe", tr, reads=[b_xn[r], b_ident], writes=[b_ptr[half]])
                        if half == 0:
                            sc.add("dve", lambda e: e.tensor_tensor(
                                out=hT[:, 0:8, s_ * 128:(s_ + 1) * 128], in0=ptr[:, 0, :].rearrange("p (j t) -> p j t", j=8),
                                in1=gmix[:, 0:8].unsqueeze(2).to_broadcast([128, 8, 128]), op=ALU.mult),
                                reads=[b_ptr[0], b_small], writes=[b_hT[hb]])
                        else:
                            sc.add("act", lambda e: _act_evac8(e, hT, ptr, gmix, 1, 1, s_), reads=[b_ptr[1], b_small], writes=[b_hT[hb]])

                def ens_L(j):
                    if j >= NSUBT or stage_done["L"] >= j:
                        return
                    ens_L(j - 1)
                    if j >= 2:
                        ens_C(j - 2)
                    st_L(j)
                    stage_done["L"] = j

                def ens_C(j):
                    if j >= NSUBT or stage_done["C"] >= j:
                        return
                    ens_C(j - 1)
                    ens_L(j)
                    if j >= 2:
                        ens_X(j - 2)
                    st_C(j)
                    stage_done["C"] = j

                def ens_X(j):
                    if j >= NSUBT or stage_done["X"] >= j:
                        return
                    ens_X(j - 1)
                    ens_C(j)
                    st_X(j)
                    stage_done["X"] = j

                def advance(kx):
                    ens_X(kx)
                    ens_C(kx + 1)
                    ens_L(kx + 2)

                advance(3)
                for tt in range(NTT):
                    hb = tt % 2
                    hT = hT2[:, hb]
                    tsl = slice(tt * 512, (tt + 1) * 512)
                    deferred = []

                    def flush_deferred(keep=1):
                        while len(deferred) > keep:
                            deferred.pop(0)()

                    parts = (debug or {}).get("parts", "fgt")
                    for n in ((debug or {}).get("frange", range(16)) if "f" in parts else []):
                        pi = next_pac()
                        piece = n // 8

                        def mm(e, n=n, pi=pi, hT=hT):
                            ins = None
                            for kc in range(NKC):
                                ins = e.matmul(pac[:, pi, :], lhsT=w1s[:, kc, n * 128:(n + 1) * 128], rhs=hT[:, kc, :],
                                               start=(kc == 0), stop=(kc == NKC - 1))
                            return ins
                        sc.add("pe", mm, reads=[b_hT[hb], b_w1p[n // 4]], writes=[b_pac[pi]])
                        flush_deferred()
                        if n % 4 == 3 and tt + 1 < NTT:
                            advance(4 * (tt + 1) + n // 4)
                        if n < 8:
                            r = chunkctr[0] % 2
                            chunkctr[0] += 1
                            gcol = 0 if n < 4 else 1
                            sc.add("dve", (lambda r=r, pi=pi: lambda e: e.tensor_copy(out=qraw[:, r, :], in_=pac[:, pi, :]))(),
                                   reads=[b_pac[pi]], writes=[b_qraw[r]])
                            sc.add("act", (lambda r=r: lambda e: e.activation(out=sq[:, r, :], in_=qraw[:, r, :], func=AF.Square))(),
                                   reads=[b_qraw[r]], writes=[b_sq[r]])

                            def post(n=n, r=r, gcol=gcol, tsl=tsl):
                                p2 = next_pac()
                                sc.add("pe", lambda e: e.matmul(pac[:, p2, :], lhsT=ones_bf[:], rhs=sq[:, r, :], start=True, stop=True),
                                       reads=[b_sq[r], b_ones], writes=[b_pac[p2]])
                                sc.add("act", lambda e: e.activation(out=rs[:, r, :], in_=pac[:, p2, :], func=AF.Sqrt, bias=epsb[:], scale=1.0 / 128),
                                       reads=[b_pac[p2], b_small], writes=[b_rs[r]])
                                sc.add("dve", lambda e: e.reciprocal(out=rsc[:, r, :], in_=rs[:, r, :]), reads=[b_rs[r]], writes=[b_rs[r]])
                                sc.add("dve", lambda e: e.scalar_tensor_tensor(out=qo[:, r, :], in0=qraw[:, r, :], scalar=qkgs[:, gcol:gcol + 1],
                                                                                in1=rsc[:, r, :], op0=ALU.mult, op1=ALU.mult),
                                       reads=[b_qraw[r], b_rs[r], b_small], writes=[b_qo[r]])
                                sc.add("sp", lambda e: e.dma_start(out=qk_scr[n, :, tsl], in_=qo[:, r, :]), reads=[b_qo[r]], dma=True)
                            deferred.append(post)
                        elif n < 12:
                            r = chunkctr[0] % 2
                            chunkctr[0] += 1
                            sc.add("act", (lambda r=r, pi=pi: lambda e: e.activation(out=mqo[:, r, :], in_=pac[:, pi, :], func=AF.Copy))(),
                                   reads=[b_pac[pi]], writes=[b_mqo[r]])
                            sc.add("sp", (lambda r=r, n=n, tsl=tsl: lambda e: e.dma_start(out=mqk_scr[n - 8, :, tsl], in_=mqo[:, r, :]))(),
                                   reads=[b_mqo[r]], dma=True)
                        else:
                            r = chunkctr[0] % 2
                            chunkctr[0] += 1
                            sc.add("act", (lambda r=r, pi=pi: lambda e: e.activation(out=moo[:, r, :], in_=pac[:, pi, :], func=AF.Sigmoid))(),
                                   reads=[b_pac[pi]], writes=[b_moo[r]])
                            sc.add("sp", (lambda r=r, n=n, tsl=tsl: lambda e: e.dma_start(out=mo_scr[n - 12, :, tsl], in_=moo[:, r, :]))(),
                                   reads=[b_moo[r]], dma=True)
                    pi = next_pac()
                    if "g" not in parts:
                        continue

                    def mmg(e, pi=pi, hT=hT):
                        ins = None
                        for kc in range(NKC):
                            ins = e.matmul(pac[0:8, pi, :], lhsT=w1s[:, kc, 3072:3080], rhs=hT[:, kc, :],
                                           start=(kc == 0), stop=(kc == NKC - 1))
                        return ins
                    sc.add("pe", mmg, reads=[b_hT[hb], b_w1[3][0], b_w1[3][1]], writes=[b_pac[pi]])
                    flush_deferred(0)
                    sc.add("dve", (lambda pi=pi, tsl=tsl: lambda e: e.tensor_copy(out=gall[:, tsl], in_=pac[0:8, pi, :]))(),
                           reads=[b_pac[pi]], writes=[b_gall])
                    for s_ in (range(4) if "t" in parts else []):
                        r = chunkctr[0] % 2
                        chunkctr[0] += 1
                        for hv in range(2):
                            pi = next_pac()

                            def mmv(e, s_=s_, hv=hv, pi=pi, hT=hT):
                                ins = None
                                for kc in range(NKC):
                                    ins = e.matmul(pac[:, pi, :], lhsT=hT[:, kc, s_ * 128:(s_ + 1) * 128],
                                                   rhs=w1s[:, kc, 2048 + hv * 512:2048 + (hv + 1) * 512],
                                                   start=(kc == 0), stop=(kc == NKC - 1))
                                return ins
                            sc.add("pe", mmv, reads=[b_hT[hb], b_w1p[4 + hv]], writes=[b_pac[pi]])
                            if hv == 0:
                                sc.add("act", (lambda r=r, pi=pi, hv=hv: lambda e: e.activation(out=vo[:, r, hv * 512:(hv + 1) * 512], in_=pac[:, pi, :], func=AF.Copy))(),
                                       reads=[b_pac[pi]], writes=[b_vo[r]])
                            else:
                                sc.add("dve", (lambda r=r, pi=pi, hv=hv: lambda e: e.tensor_copy(out=vo[:, r, hv * 512:(hv + 1) * 512], in_=pac[:, pi, :]))(),
                                       reads=[b_pac[pi]], writes=[b_vo[r]])
                        row0 = tt * 512 + s_ * 128
                        sc.add("sp", (lambda r=r, row0=row0: lambda e: e.dma_start(out=v_scr[row0:row0 + 128, :], in_=vo[:, r, :]))(),
                               reads=[b_vo[r]], dma=True)
                run_block()
            if debug is not None and debug.get("stop") == 1:
                gs.close()
                _finish(nc, sc, es, y, xo, run_block)
                return nc
            if not only4:
                _phase2(nc, sc, es, run_block, debug, dict(
                    gall=gall, b_gall=b_gall, ident_f=ident_f, ident_bf=ident_bf, ones_bf=ones_bf, b_ident=b_ident, b_ones=b_ones,
                    pac=pac, b_pac=b_pac, ptr=ptr, b_ptr=b_ptr, bias8=bias8, g_scr=g_scr, m_scr=m_scr, qk_scr=qk_scr,
                    mqk_scr=mqk_scr, mo_scr=mo_scr, v_scr=v_scr, convw=convw, convb=convb, onw=onw, bounce=bounce,
                    bounce_t=bounce_t, gathered_t=gathered_t, s_cc=s_cc, b_gathered=b_gathered, emit_conv=emit_conv, conv_list=conv_list))
            if only4:
                run_block()
            gs.close()
            if debug is not None and debug.get("stop") == 2:
                _finish(nc, sc, es, y, xo, run_block)
                return nc
            emit_conv(len(conv_list))
            dbg_cat = None
            if debug is not None and not debug.get("exchange", True):
                dbg_cat = nc.dram_tensor("dbg_cat", [D, 2048], BF16, kind="ExternalInput").ap()
            _phase4(nc, sc, es, run_block, debug, dict(
                ident_bf=ident_bf, ones_bf=ones_bf, b_ident=b_ident, b_ones=b_ones, pac=pac, b_pac=b_pac, ptr=ptr, b_ptr=b_ptr,
                gathered_t=gathered_t, xo=xo, po=po, wout=wout, wnffn=wnffn, wg=wg, wu=wu, wd=wd, wnple=wnple, wpg=wpg, wpp=wpp,
                wpost=wpost, y=y, cc_buf=b_gathered, dbg_cat=dbg_cat, wb=wb, b_wb=b_wb))
    return nc


def _phase2(nc, sc, es, run_block, debug, A):
    import contextlib
    gall = A["gall"]; pac = A["pac"]; b_pac = A["b_pac"]; g_scr = A["g_scr"]; m_scr = A["m_scr"]
    ident_f = A["ident_f"]; ones_bf = A["ones_bf"]; b_ident = A["b_ident"]; b_ones = A["b_ones"]
    qk_scr = A["qk_scr"]; mqk_scr = A["mqk_scr"]; mo_scr = A["mo_scr"]; v_scr = A["v_scr"]; bounce = A["bounce"]
    b_gall = A["b_gall"]
    nJ = 8 if debug is None else debug.get("nJ", 8)
    heads_a = range(4) if debug is None else range(debug.get("nha", 4))
    heads_m = range(2) if debug is None else range(debug.get("nhm", 2))
    pers = contextlib.ExitStack()
    with pers:
        P = lambda name, shape, dt: pers.enter_context(nc.sbuf_tensor(name, shape, dt))
        negc = P("negc", [128, 32, 4], F32)
        acol = P("acol", [128, 32, 2], F32)
        negMbc = P("negMbc", [128, 16], F32)
        mqkT = P("mqkT", [128, 4, S], BF16)
        epsb = P("epsb2", [128, 1], F32)
        c_pad = P("c_pad", [128, S], BF16)
        sel = P("sel", [128, 4, 128], BF16)
        mneg = P("mneg", [128, 4, 512], BF16)
        m01 = P("m01", [128, 4, 512], BF16)
        negMdk = P("negMdk", [128, 16], F32)
        b_cpad = Buf(); b_sel = Buf(); b_mneg = Buf(); b_m01 = Buf(); b_negMdk = Buf()
        b_negc = Buf(); b_acol = Buf(); b_negM = Buf(); b_eps = Buf()
        RS = 128.0 ** 0.5
        sc.add("pool", lambda e: e.memset(c_pad[:], 0.0), writes=[b_cpad])
        sc.add("pool", lambda e: e.memset(sel[:], 1.0), writes=[b_sel])
        sc.add("pool", lambda e: e.memset(mneg[:], 0.0), writes=[b_mneg])
        sc.add("pool", lambda e: e.memset(m01[:], 1.0), writes=[b_m01])
        for a_ in range(4):
            sc.add("pool", (lambda a_=a_: lambda e: e.affine_select(out=sel[:, a_, :], in_=sel[:, a_, :], pattern=[[0, 128]], compare_op=ALU.is_equal,
                                                                   fill=_fillreg(e, 0.0), base=-a_, channel_multiplier=1))(), writes=[b_sel])
            sc.add("pool", (lambda a_=a_: lambda e: e.affine_select(out=mneg[:, a_, :], in_=mneg[:, a_, :], pattern=[[1, 512]], compare_op=ALU.is_ge,
                                                                   fill=_fillreg(e, NEG * RS), base=-128 * a_, channel_multiplier=-1))(), writes=[b_mneg])
            sc.add("pool", (lambda a_=a_: lambda e: e.affine_select(out=m01[:, a_, :], in_=m01[:, a_, :], pattern=[[1, 512]], compare_op=ALU.is_ge,
                                                                   fill=_fillreg(e, 0.0), base=-128 * a_, channel_multiplier=-1))(), writes=[b_m01])
        b_mqkT = [Buf() for _ in range(4)]
        b_gscr = Buf(); b_mscr = Buf()
        sc.add("pool", lambda e: e.memset(epsb[:], EPS), writes=[b_eps])
        ph = contextlib.ExitStack()
        with ph:
            T = lambda name, shape, dt: ph.enter_context(nc.sbuf_tensor(name, shape, dt))
            b8 = T("b8", [8, 1], F32)
            one8 = T("one8", [8, 1], F32)
            e8 = T("e8", [8, S], F32)
            cs8 = T("cs8", [8, S], F32)
            z8 = T("z8", [8, S], F32)
            A2 = T("A2", [2, S], F32)
            F2 = T("F2", [2, S], F32)
            tmax = T("tmax", [2, 16], F32)
            ucol = T("ucol", [128, 32, 8], F32)
            ccol = T("ccol", [128, 32, 8], F32)
            b_b8 = Buf(); b_e8 = Buf(); b_cs8 = Buf(); b_z8 = Buf(); b_A2 = Buf(); b_F2 = Buf(); b_tm = Buf()
            b_ucol = Buf(); b_ccol = Buf()
            sc.add("sp", lambda e: e.dma_start(out=b8[:], in_=A["bias8"]), writes=[b_b8], dma=True)
            sc.add("pool", lambda e: e.memset(one8[:], 1.0), writes=[b_b8])
            sc.add("pool", lambda e: e.memset(z8[:], 0.0), writes=[b_z8])
            sc.add("dve", lambda e: e.tensor_scalar(out=gall[:], in0=gall[:], scalar1=b8[:], scalar2=None, op0=ALU.add),
                   reads=[b_b8], writes=[b_gall])
            sc.add("act", lambda e: e.activation(out=e8[:], in_=gall[:], func=AF.Exp, scale=-1.0), reads=[b_gall], writes=[b_e8])
            sc.add("act", lambda e: e.activation(out=e8[:], in_=e8[:], func=AF.Ln, bias=one8[:], scale=1.0), reads=[b_b8], writes=[b_e8])
            sc.add("dve", lambda e: e.tensor_scalar(out=e8[:], in0=e8[:], scalar1=-1.0, scalar2=None, op0=ALU.mult), writes=[b_e8])
            sc.add("dve", lambda e: e.tensor_tensor_scan(out=cs8[:], data0=e8[:], data1=z8[:], initial=0.0, op0=ALU.add, op1=ALU.add),
                   reads=[b_e8, b_z8], writes=[b_cs8])
            sc.add("dve", lambda e: e.tensor_scalar(out=c_pad[0:4, :], in0=cs8[0:4, :], scalar1=RS, scalar2=None, op0=ALU.mult), reads=[b_cs8], writes=[b_cpad])
            sc.add("sp", lambda e: e.dma_start(out=g_scr[0:8, :], in_=gall[:]), reads=[b_gall], writes=[b_gscr], dma=True)
            sc.add("sp", lambda e: e.dma_start(out=g_scr[8:16, :], in_=cs8[:]), reads=[b_cs8], writes=[b_gscr], dma=True)
            for src, bsrc, dst, bdst, bank in ((gall, b_gall, ucol, b_ucol, 0), (cs8, b_cs8, ccol, b_ccol, 1)):
                def trg(e, src=src, bank=bank):
                    ins = None
                    for blk in range(32):
                        ins = e.transpose(out=pac[:, bank, blk * 8:(blk + 1) * 8], in_=src[:, blk * 128:(blk + 1) * 128],
                                          identity=ident_f[0:8, 0:8])
                    return ins
                sc.add("pe", trg, reads=[bsrc, b_ident], writes=[b_pac[bank]])
                sc.add("dve", (lambda dst=dst, bank=bank: lambda e: e.tensor_copy(out=dst[:].rearrange("p b g -> p (b g)"), in_=pac[:, bank, 0:256]))(),
                       reads=[b_pac[bank]], writes=[bdst])
            sc.add("dve", lambda e: e.tensor_scalar(out=negc[:], in0=ccol[:, :, 0:4], scalar1=-1.0, scalar2=None, op0=ALU.mult),
                   reads=[b_ccol], writes=[b_negc])
            sc.add("dve", lambda e: e.tensor_tensor(out=acol[:], in0=ucol[:, :, 4:6], in1=ccol[:, :, 6:8], op=ALU.subtract),
                   reads=[b_ucol, b_ccol], writes=[b_acol])
            sc.add("sp", lambda e: e.dma_start(out=A2[:], in_=g_scr[4:6, :]), reads=[b_gscr], writes=[b_A2], dma=True)
            sc.add("sp", lambda e: e.dma_start(out=F2[:], in_=g_scr[14:16, :]), reads=[b_gscr], writes=[b_F2], dma=True)
            sc.add("dve", lambda e: e.tensor_tensor(out=A2[:], in0=A2[:], in1=F2[:], op=ALU.subtract), reads=[b_F2], writes=[b_A2])
            sc.add("dve", lambda e: e.tensor_reduce(out=tmax[:, 0:8], in_=A2[:].rearrange("p (j t) -> p j t", j=8), axis=AX.X, op=ALU.max),
                   reads=[b_A2], writes=[b_tm])
            sc.add("dve", lambda e: e.tensor_tensor_scan(out=tmax[:, 8:16], data0=tmax[:, 0:8], data1=tmax[:, 0:8], initial=-1e30,
                                                          op0=ALU.max, op1=ALU.max), writes=[b_tm])
            sc.add("dve", lambda e: e.tensor_scalar(out=tmax[:, 0:8], in0=tmax[:, 8:16], scalar1=-1.0, scalar2=None, op0=ALU.mult), writes=[b_tm])
            sc.add("sp", lambda e: e.dma_start(out=m_scr.rearrange("o (h j) -> (o h) j", h=2), in_=tmax[:, 0:8]), reads=[b_tm], writes=[b_mscr], dma=True)
            sc.add("sp", lambda e: e.dma_start(out=negMbc[:], in_=m_scr[0:1, :].partition_broadcast(128).rearrange("p o n -> p (o n)")),
                   reads=[b_mscr], writes=[b_negM], dma=True)
            sc.add("dve", lambda e: e.tensor_scalar(out=negMdk[:], in0=negMbc[:], scalar1=float(np.log(128.0 ** -0.5)), scalar2=None, op0=ALU.add), reads=[b_negM], writes=[b_negMdk])
            run_block()
        A["emit_conv"](len(A["conv_list"]))
        ph = contextlib.ExitStack()
        with ph:
            T = lambda name, shape, dt: ph.enter_context(nc.sbuf_tensor(name, shape, dt))
            qT = T("qT", [128, S], BF16)
            kT = T("kT", [128, S], BF16)
            Vh = T("Vh", [128, 32, 256], BF16)
            Cbc = T("Cbc", [128, 1, S], F32)
            moT = T("moT", [128, 2, S], BF16)
            NR = 5
            LA = 3
            raw = T("raw", [128, 2, S], F32)
            acc = T("acc", [128, S], F32)
            cw = T("cw", [128, 4, 4], F32)
            cb = T("cb", [128, 4], F32)
            b_raw = [Buf(), Buf()]; b_acc = Buf(); b_cw = Buf()
            sc.add("sp", lambda e: e.dma_start(out=cw[:], in_=A["convw"]), writes=[b_cw], dma=True)
            sc.add("sp", lambda e: e.dma_start(out=cb[:], in_=A["convb"]), writes=[b_cw], dma=True)
            conv_ops = []
            for ch in range(4):
                r_ = ch % 2
                conv_ops.append((lambda ch=ch, r_=r_: sc.add("sp", lambda e: e.dma_start(out=raw[:, r_, :], in_=mqk_scr[ch]), writes=[b_raw[r_]], dma=True)))
                conv_ops.append((lambda ch=ch, r_=r_: sc.add("dve", lambda e: e.tensor_scalar(out=acc[:], in0=raw[:, r_, :], scalar1=cw[:, ch, 3:4], scalar2=None, op0=ALU.mult),
                                                           reads=[b_raw[r_], b_cw], writes=[b_acc])))
                for sh in (1, 2, 3):
                    conv_ops.append((lambda ch=ch, r_=r_, sh=sh: sc.add("dve", lambda e: e.scalar_tensor_tensor(
                        out=acc[:, sh:], in0=raw[:, r_, 0:S - sh], scalar=cw[:, ch, 3 - sh:4 - sh], in1=acc[:, sh:], op0=ALU.mult, op1=ALU.add),
                        reads=[b_raw[r_]], writes=[b_acc])))
                conv_ops.append((lambda ch=ch: sc.add("act", lambda e: e.activation(out=mqkT[:, ch, :], in_=acc[:], func=AF.Silu, bias=cb[:, ch:ch + 1], scale=1.0),
                                                      reads=[b_acc, b_cw], writes=[b_mqkT[ch]])))
            conv_ops.pop(0)()
            pT = T("pT", [128, NR, 512], BF16)
            rz = T("rz", [128, 512], F32)
            rsc2 = T("rsc2", [128, 512], F32)
            rn2 = T("rn2", [128, 512], F32)
            ao = T("ao", [128, 2, 512], BF16)
            wc = T("wc", [128, 2, 32], F32)
            bb = T("bb", [128, 512], F32)
            hb = T("hb", [128, 2, 512], F32)
            sqh = T("sqh", [128, 2, 512], BF16)
            rn = T("rn", [128, 512], F32)
            onws = T("onws", [128, 4], F32)
            b_qT = Buf(); b_kT = Buf(); b_Vh = Buf(); b_Cbc = Buf(); b_moT = Buf()
            b_qT0 = Buf(); b_kT0 = Buf(); b_Vh0 = Buf()
            b_tmp = [Buf() for _ in range(NR)]; b_pT = [Buf() for _ in range(NR)]
            SB = [pac[:, 0, :], pac[:, 1, :], A["ptr"][:, 0, :].bitcast(F32), A["ptr"][:, 1, :].bitcast(F32), pac[:, 2, :], pac[:, 3, :]]
            b_SB = [b_pac[0], b_pac[1], A["b_ptr"][0], A["b_ptr"][1], b_pac[2], b_pac[3]]
            b_rz = Buf(); b_ao = [Buf(), Buf()]; b_wc = [Buf(), Buf()]; b_bb = Buf(); b_hb = Buf(); b_sqh = Buf(); b_rn = Buf()
            b_onw = Buf(); b_bounce = [Buf() for _ in range(8)]
            sc.add("sp", lambda e: e.dma_start(out=onws[:], in_=A["onw"]), writes=[b_onw], dma=True)
            do_exch = debug is None or debug.get("exchange", True)

            def exchange(c):
                if not do_exch:
                    return
                op = sc.add("pool", lambda e: e.collective_compute("AllGather", ALU.bypass, replica_groups=[[0, 1], [2, 3], [4, 5], [6, 7]],
                                                                   ins=[A["bounce_t"][c].ap().opt()], outs=[A["gathered_t"][c].ap().opt()]),
                            reads=[b_bounce[c]], writes=[A["b_gathered"][c]], dma=True)
                sc.ndma["pool"] -= 1
                sc.dma_ops["pool"].pop()
                sc.cc_ops.append(op)
                op.dsem = A["s_cc"]; op.dval = len(sc.cc_ops); op.prev_dma = None; op.cc = True
            scale = 128.0 ** -0.5
            rot = [0]
            aoc = [0]
            def do_attn(h):
                vsrc = v_scr[:, h * 128:(h + 1) * 128].rearrange("(b p) d -> p b d", p=128)
                sc.add("sp", (lambda h=h: lambda e: e.dma_start(out=qT[:, 0:512], in_=qk_scr[h][:, 0:512]))(), writes=[b_qT0], dma=True)
                sc.add("sp", (lambda h=h: lambda e: e.dma_start(out=kT[:, 0:512], in_=qk_scr[4 + h][:, 0:512]))(), writes=[b_kT0], dma=True)
                sc.add("sp", (lambda vsrc=vsrc: lambda e: e.dma_start(out=Vh[:, 0:4, 0:128], in_=vsrc[:, 0:4, :]))(), writes=[b_Vh0], dma=True)
                sc.add("sp", (lambda h=h: lambda e: e.dma_start(out=qT[:, 512:S], in_=qk_scr[h][:, 512:S]))(), writes=[b_qT], dma=True)
                sc.add("sp", (lambda h=h: lambda e: e.dma_start(out=kT[:, 512:S], in_=qk_scr[4 + h][:, 512:S]))(), writes=[b_kT], dma=True)
                sc.add("sp", (lambda vsrc=vsrc: lambda e: e.dma_start(out=Vh[:, 4:32, 0:128], in_=vsrc[:, 4:32, :]))(), writes=[b_Vh], dma=True)
                tiles = [(J, i) for J in range(nJ) for i in range(4 * J + 4)]
                pend = None

                def issue_S(J, i):
                    sb = rot[0] % 6
                    r = rot[0] % NR
                    rot[0] += 1
                    a_ = i - 4 * J

                    def smm(e, h=h):
                        e.matmul(SB[sb], lhsT=kT[:, i * 128:(i + 1) * 128], rhs=qT[:, J * 512:(J + 1) * 512], start=True, stop=False)
                        ins = e.matmul(SB[sb], lhsT=sel[:, h, :], rhs=c_pad[:, J * 512:(J + 1) * 512], start=False, stop=(a_ < 0))
                        if a_ >= 0:
                            ins = e.matmul(SB[sb], lhsT=A["ident_bf"][:], rhs=mneg[:, a_, :], start=False, stop=True)
                        return ins
                    sc.add("pe", smm, reads=[b_qT0 if J == 0 else b_qT, b_kT0 if i < 4 else b_kT, b_cpad, b_sel, b_mneg, b_ident], writes=[b_SB[sb]])
                    return sb, r

                def rest(J, i, sb, r, h=h):
                    ob = 4
                    zb = 5
                    last = 4 * J + 3
                    sc.add("act", lambda e: e.activation(out=pT[:, r, :], in_=SB[sb], func=AF.Exp, bias=negc[:, i, h:h + 1], scale=scale),
                           reads=[b_SB[sb], b_negc], writes=[b_pT[r]])

                    def pv(e):
                        e.matmul(pac[:, ob, :], lhsT=Vh[:, i, 0:128], rhs=pT[:, r, :], start=(i == 0), stop=(i == last))
                        return e.matmul(pac[:, zb, :], lhsT=ones_bf[:], rhs=pT[:, r, :], start=(i == 0), stop=(i == last))
                    sc.add("pe", pv, reads=[b_pT[r], b_Vh0 if i < 4 else b_Vh, b_ones], writes=[b_pac[ob], b_pac[zb]])
                    if i == last:
                        a = aoc[0] % 2
                        aoc[0] += 1
                        sc.add("act", lambda e: e.activation(out=hb[:, 0, :], in_=pac[:, ob, :], func=AF.Copy), reads=[b_pac[ob]], writes=[b_hb])
                        sc.add("dve", lambda e: e.tensor_copy(out=rsc2[:], in_=pac[:, zb, :]), reads=[b_pac[zb]], writes=[b_rz])
                        sc.add("dve", lambda e: e.reciprocal(out=rz[:], in_=rsc2[:]), writes=[b_rz])
                        sc.add("dve", lambda e: e.tensor_tensor(out=ao[:, a, :], in0=hb[:, 0, :], in1=rz[:], op=ALU.mult),
                               reads=[b_hb, b_rz], writes=[b_ao[a]])
                        sc.add("sp", lambda e: e.dma_start(out=bounce[h][:, J * 512:(J + 1) * 512], in_=ao[:, a, :]),
                               reads=[b_ao[a]], writes=[b_bounce[h]], dma=True)
                        for _ in range(2 if J >= 2 else 0):
                            if conv_ops:
                                conv_ops.pop(0)()
                pendq = []
                for (J, i) in tiles:
                    pendq.append((J, i) + issue_S(J, i))
                    if len(pendq) > 5:
                        rest(*pendq.pop(0))
                while pendq:
                    rest(*pendq.pop(0))
                exchange(h)
            dk = 128.0 ** -0.5

            def do_mlstm(hp):
                while conv_ops:
                    conv_ops.pop(0)()
                vsrc2 = v_scr[:, 512 + hp * 256:512 + (hp + 1) * 256].rearrange("(b p) d -> p b d", p=128)
                sc.add("sp", (lambda vsrc2=vsrc2: lambda e: e.dma_start(out=Vh[:, 0:4, :], in_=vsrc2[:, 0:4, :]))(), writes=[b_Vh0], dma=True)
                sc.add("sp", (lambda vsrc2=vsrc2: lambda e: e.dma_start(out=Vh[:, 4:32, :], in_=vsrc2[:, 4:32, :]))(), writes=[b_Vh], dma=True)
                sc.add("sp", (lambda hp=hp: lambda e: e.dma_start(out=moT[:], in_=mo_scr[2 * hp:2 * hp + 2].rearrange("c p s -> p c s")))(),
                       writes=[b_moT], dma=True)
                sc.add("sp", (lambda hp=hp: lambda e: e.dma_start(out=Cbc[:], in_=g_scr[14 + hp:15 + hp, :].partition_broadcast(128)))(),
                       reads=[b_gscr], writes=[b_Cbc], dma=True)
                qTm = mqkT[:, hp, :]
                kTm = mqkT[:, 2 + hp, :]
                tiles = [(J, i) for J in range(nJ) for i in range(4 * J + 4)]
                pend = None

                def issue_S2(J, i, hp=hp, qTm=qTm, kTm=kTm):
                    sb = rot[0] % 4
                    r = rot[0] % NR
                    rot[0] += 1
                    sc.add("pe", lambda e: e.matmul(SB[sb], lhsT=kTm[:, i * 128:(i + 1) * 128], rhs=qTm[:, J * 512:(J + 1) * 512], start=True, stop=True),
                           reads=[b_mqkT[hp], b_mqkT[2 + hp]], writes=[b_SB[sb]])
                    return sb, r

                tails = []
                tailsS = []
                tailsB = []

                def emit_wc(J, hp=hp):
                    w_ = J % 2
                    lastJ = 4 * J + 3
                    mc = hp * 8 + J
                    sc.add("act", lambda e: e.activation(out=wc[:, w_, 0:lastJ + 1], in_=acol[:, 0:lastJ + 1, hp], func=AF.Exp, bias=negMdk[:, mc:mc + 1], scale=1.0),
                           reads=[b_acol, b_negMdk], writes=[b_wc[w_]])

                def rest2(J, i, sb, r, hp=hp):
                    last = 4 * J + 3
                    w = J % 2
                    mcol = hp * 8 + J
                    if i < 4 * J:
                        sc.add("act", lambda e: e.activation(out=pT[:, r, :], in_=SB[sb], func=AF.Copy, scale=wc[:, w, i:i + 1]),
                               reads=[b_SB[sb], b_wc[w]], writes=[b_pT[r]])
                    elif i < 4 * J:
                        sc.add("dve", lambda e: e.tensor_scalar(out=pT[:, r, :], in0=SB[sb], scalar1=wc[:, w, i:i + 1], scalar2=None, op0=ALU.mult),
                               reads=[b_SB[sb], b_wc[w]], writes=[b_pT[r]])
                    else:
                        sc.add("dve", lambda e: e.scalar_tensor_tensor(out=pT[:, r, :], in0=SB[sb], scalar=wc[:, w, i:i + 1], in1=m01[:, i - 4 * J, :],
                                                                        op0=ALU.mult, op1=ALU.mult),
                               reads=[b_SB[sb], b_wc[w], b_m01], writes=[b_pT[r]])

                    def pv(e):
                        e.matmul(pac[:, 2, :], lhsT=Vh[:, i, 0:128], rhs=pT[:, r, :], start=(i == 0), stop=(i == last))
                        e.matmul(pac[:, 3, :], lhsT=Vh[:, i, 128:256], rhs=pT[:, r, :], start=(i == 0), stop=(i == last))
                        return e.matmul(pac[:, 4, :], lhsT=ones_bf[:], rhs=pT[:, r, :], start=(i == 0), stop=(i == last))
                    sc.add("pe", pv, reads=[b_pT[r], b_Vh0 if i < 4 else b_Vh, b_ones], writes=[b_pac[2], b_pac[3], b_pac[4]])
                    if i == 0 and tails:
                        tails.pop(0)()
                    if i == min(6, 4 * J - 1) and tailsS:
                        tailsS.pop(0)()
                    if i == min(9, 4 * J) and tailsB:
                        tailsB.pop(0)()
                    if i == last:
                        Js = slice(J * 512, (J + 1) * 512)
                        sc.add("act", lambda e: e.activation(out=rz[:], in_=pac[:, 4, :], func=AF.Abs), reads=[b_pac[4]], writes=[b_rz])
                        for c in range(2):
                            sc.add("dve", (lambda c=c: lambda e: e.tensor_copy(out=hb[:, c, :], in_=pac[:, 2 + c, :]))(), reads=[b_pac[2 + c]], writes=[b_hb])

                        def tail(J=J, Js=Js, mcol=mcol, hp=hp):
                            sc.add("act", lambda e: e.activation(out=bb[:], in_=Cbc[:, 0, Js], func=AF.Exp, bias=negMbc[:, mcol:mcol + 1], scale=-1.0),
                                   reads=[b_Cbc, b_negM], writes=[b_bb])
                            sc.add("dve", lambda e: e.tensor_tensor(out=rz[:], in0=rz[:], in1=bb[:], op=ALU.max), reads=[b_bb], writes=[b_rz])
                            sc.add("dve", lambda e: e.reciprocal(out=rsc2[:], in_=rz[:]), writes=[b_rz])
                            for c in range(2):
                                sc.add("dve", (lambda c=c: lambda e: e.tensor_tensor(out=hb[:, c, :], in0=hb[:, c, :], in1=rsc2[:], op=ALU.mult))(),
                                       reads=[b_rz], writes=[b_hb])
                            tailsS.append(lambda: sc.add("act", lambda e: e.activation(out=sqh[:].rearrange("p c t -> p (c t)"), in_=hb[:].rearrange("p c t -> p (c t)"), func=AF.Square),
                                                         reads=[b_hb], writes=[b_sqh]))
                            tailsB.append(lambda: tailB())

                        def tailB(J=J, Js=Js, mcol=mcol, hp=hp):
                            def ssq(e):
                                e.matmul(pac[:, 5, :], lhsT=ones_bf[:], rhs=sqh[:, 0, :], start=True, stop=False)
                                return e.matmul(pac[:, 5, :], lhsT=ones_bf[:], rhs=sqh[:, 1, :], start=False, stop=True)
                            sc.add("pe", ssq, reads=[b_sqh, b_ones], writes=[b_pac[5]])
                            sc.add("act", lambda e: e.activation(out=rn[:], in_=pac[:, 5, :], func=AF.Sqrt, bias=epsb[:], scale=1.0 / 256),
                                   reads=[b_pac[5], b_eps], writes=[b_rn])
                            sc.add("dve", lambda e: e.reciprocal(out=rn2[:], in_=rn[:]), writes=[b_rn])
                            for c in range(2):
                                a = aoc[0] % 2
                                aoc[0] += 1
                                oc = 2 * hp + c
                                sc.add("dve", (lambda c=c, oc=oc: lambda e: e.scalar_tensor_tensor(out=hb[:, c, :], in0=hb[:, c, :], scalar=onws[:, oc:oc + 1], in1=rn2[:],
                                                                                                  op0=ALU.mult, op1=ALU.mult))(),
                                       reads=[b_rn, b_onw], writes=[b_hb])
                                sc.add("dve", (lambda c=c, a=a: lambda e: e.tensor_tensor(out=ao[:, a, :], in0=hb[:, c, :], in1=moT[:, c, Js], op=ALU.mult))(),
                                       reads=[b_hb, b_moT], writes=[b_ao[a]])
                                sc.add("sp", (lambda a=a, oc=oc: lambda e: e.dma_start(out=bounce[4 + oc][:, Js], in_=ao[:, a, :]))(),
                                       reads=[b_ao[a]], writes=[b_bounce[4 + oc]], dma=True)
                        tails.append(tail)
                        if J + 1 < nJ:
                            emit_wc(J + 1)
                emit_wc(0)
                pendq = []
                for (J, i) in tiles:
                    pendq.append((J, i) + issue_S2(J, i))
                    if len(pendq) > LA:
                        rest2(*pendq.pop(0))
                while pendq:
                    rest2(*pendq.pop(0))
                while tails or tailsS or tailsB:
                    if tails:
                        tails.pop(0)()
                    if tailsS:
                        tailsS.pop(0)()
                    if tailsB:
                        tailsB.pop(0)()
                exchange(4 + 2 * hp)
                exchange(5 + 2 * hp)

            la_ = list(heads_a)
            for h_ in la_[:len(la_) // 2]:
                do_attn(h_)
            for hp_ in heads_m:
                do_mlstm(hp_)
            for h_ in la_[len(la_) // 2:]:
                do_attn(h_)
            run_block()


def _phase4(nc, sc, es, run_block, debug, A):
    import contextlib
    pac = A["pac"]; b_pac = A["b_pac"]; ptr = A["ptr"]; b_ptr = A["b_ptr"]
    ident_bf = A["ident_bf"]; b_ident = A["b_ident"]
    xo = A["xo"]; po = A["po"]; y = A["y"]
    ntk = 4 if debug is None else debug.get("ntk", 4)
    use_gather = debug is None or debug.get("exchange", True)
    ph = contextlib.ExitStack()
    with ph:
        T = lambda name, shape, dt: ph.enter_context(nc.sbuf_tensor(name, shape, dt))
        xs = T("xs", [128, 4, D], F32)
        hT = T("hT4", [128, NKC, 512], BF16)
        U = T("U", [128, 32768], BF16)
        concatT = T("catT", [128, 16, 512], BF16)
        actT = U[:, 0:22528].rearrange("p (j t) -> p j t", j=44)
        gt = U[:, 0:16384].bitcast(F32).rearrange("p (s d) -> p s d", s=4)
        et = U[:, 16384:32768].bitcast(F32).rearrange("p (s d) -> p s d", s=4)
        WP = T("WP", [128, 3, 8192], BF16)
        xn = T("xn4", [128, 2, D], BF16)
        stat = T("stat4", [128, 8], F32)
        gffn = T("gffn", [128, NKC], F32)
        gple = T("gple", [128, NKC], F32)
        wpb = T("wpb", [128, 1, D], F32)
        ppT = T("ppT", [128, 2, 512], BF16)
        pt = T("pt", [128, 2, 256], F32)
        ptb = T("ptb", [128, 2, 256], BF16)
        sg = T("sg", [128, 2, 512], F32)
        sqj = sg[:].rearrange("p a b -> p (a b)").bitcast(BF16)
        wppT = T("wppT", [128, 2, 2, 512], BF16)
        epsb = T("epsb4", [128, 1], F32)
        b_xs = [Buf() for _ in range(4)]
        b_hT = Buf(); b_cat = Buf(); b_fU = Buf()
        b_actT = [Buf() for _ in range(44)]
        b_gt = [Buf() for _ in range(4)]; b_et = [Buf() for _ in range(4)]
        b_wp = [Buf() for _ in range(3)]
        b_xn = [Buf(), Buf()]; b_stat = [Buf(), Buf()]; b_small = Buf()
        b_ppT = Buf(); b_pt = [Buf(), Buf()]; b_ptb = [Buf(), Buf()]; b_sg = [Buf(), Buf()]; b_wpp = [Buf(), Buf()]
        sc.add("sp", lambda e: e.dma_start(out=gffn[:], in_=A["wnffn"]), writes=[b_small], dma=True)
        sc.add("sp", lambda e: e.dma_start(out=gple[:], in_=A["wnple"]), writes=[b_small], dma=True)
        sc.add("sp", lambda e: e.dma_start(out=wpb[:], in_=A["wpost"].partition_broadcast(128)), writes=[b_small], dma=True)
        WBv = {k: v.rearrange("(c p) n -> p c n", p=128) for k, v in A["wb"].items()}
        woutv, wgv, wuv, wdv, wpgv, wppv = (WBv[k] for k in ("wout", "wg", "wu", "wd", "wpg", "wpp"))
        b_wsrc = {id(WBv[k]): A["b_wb"][k] for k in WBv}
        conv_done = Buf()
        sc.add("pool", lambda e: e.memset(epsb[:], EPS), writes=[conv_done, b_small] + list(A["b_wb"].values()))
        slotc = [0]
        pacn = [0]
        cnt = [0]

        def next_pac():
            i = pacn[0] % 6
            pacn[0] += 1
            return i

        def load_panel(src, nck):
            sl = slotc[0] % 3
            slotc[0] += 1
            view = WP[:, sl, 0:nck * 512].rearrange("p (c n) -> p c n", c=nck)
            sc.add("sp", lambda e: e.dma_start(out=view, in_=src), reads=[conv_done], writes=[b_wp[sl]], dma=True)
            return view, b_wp[sl]

        def norm_transpose(gain):
            base = cnt[0]
            cnt[0] += 4

            def C(s_):
                r = (base + s_) % 2
                sc.add("act", lambda e: e.activation(out=sqj[:], in_=xs[:, s_, :], func=AF.Square, accum_out=stat[:, 4 * r:4 * r + 1]),
                       reads=[b_xs[s_]], writes=[b_sg[0], b_sg[1], b_stat[r]])
                sc.add("act", lambda e: e.activation(out=stat[:, 4 * r + 1:4 * r + 2], in_=stat[:, 4 * r:4 * r + 1], func=AF.Sqrt, bias=epsb[:], scale=1.0 / D),
                       reads=[b_small], writes=[b_stat[r]])
                sc.add("dve", lambda e: e.reciprocal(out=stat[:, 4 * r + 2:4 * r + 3], in_=stat[:, 4 * r + 1:4 * r + 2]), writes=[b_stat[r]])
                sc.add("act", lambda e: e.activation(out=xn[:, r, :], in_=xs[:, s_, :], func=AF.Copy, scale=stat[:, 4 * r + 2:4 * r + 3]),
                       reads=[b_xs[s_], b_stat[r]], writes=[b_xn[r]])

            def X(s_):
                r = (base + s_) % 2
                for half in range(2):
                    def tr(e, half=half):
                        ins = None
                        for j in range(8):
                            kc = half * 8 + j
                            ins = e.transpose(out=ptr[:, half, j * 128:(j + 1) * 128], in_=xn[:, r, kc * 128:(kc + 1) * 128], identity=ident_bf[:])
                        return ins
                    sc.add("pe", tr, reads=[b_xn[r], b_ident], writes=[b_ptr[half]])
                    sc.add("dve", (lambda half=half: lambda e: e.tensor_tensor(
                        out=hT[:, half * 8:(half + 1) * 8, s_ * 128:(s_ + 1) * 128],
                        in0=ptr[:, half, :].rearrange("p (j t) -> p j t", j=8),
                        in1=gain[:, half * 8:(half + 1) * 8].unsqueeze(2).to_broadcast([128, 8, 128]), op=ALU.mult))(),
                        reads=[b_ptr[half], b_small], writes=[b_hT])
            C(0); C(1); X(0); C(2); X(1); C(3); X(2); X(3)

        def load_concat(tk):
            tsl = slice(tk * 512, (tk + 1) * 512)
            if use_gather:
                for c in (0, 1, 4, 5, 6, 7, 2, 3):
                    def ldcat(e, tsl=tsl, c=c):
                        if "rank" not in _REG:
                            _REG["rank"] = e.partition_id() % 2
                        rank = _REG["rank"]
                        gv = A["gathered_t"][c].ap().rearrange("(r p) (h n) -> p r h n", p=128, h=2)
                        return e.dma_start(out=concatT[:, c::8, :].unsqueeze(2), in_=gv[:, :, bass.ds(rank, 1), tsl])
                    sc.add("pool", ldcat, reads=[A["cc_buf"][c]], writes=[b_cat], dma=True)
            else:
                bv = A["dbg_cat"].rearrange("(c p) n -> p c n", p=128)
                sc.add("sp", (lambda tsl=tsl: lambda e: e.dma_start(out=concatT[:], in_=bv[:, :, tsl]))(), writes=[b_cat], dma=True)

        wout_pref = []
        for tk in range(ntk):
            tsl = slice(tk * 512, (tk + 1) * 512)
            for s_ in range(4):
                r0 = tk * 512 + s_ * 128
                sc.add("sp", (lambda s_=s_, r0=r0: lambda e: e.dma_start(out=xs[:, s_, :], in_=xo[r0:r0 + 128, :]))(), writes=[b_xs[s_]], dma=True)
            if tk == 0:
                load_concat(0)
            for nch in range(4):
                csl = slice(nch * 512, (nch + 1) * 512)
                if wout_pref:
                    Wp, bw = wout_pref.pop(0)
                else:
                    Wp, bw = load_panel(woutv[:, :, csl], 16)
                for s_ in range(4):
                    pi = next_pac()

                    def mm(e, s_=s_, pi=pi, Wp=Wp):
                        ins = None
                        for kc in range(NKC):
                            ins = e.matmul(pac[:, pi, :], lhsT=concatT[:, kc, s_ * 128:(s_ + 1) * 128], rhs=Wp[:, kc, :], start=(kc == 0), stop=(kc == NKC - 1))
                        return ins
                    sc.add("pe", mm, reads=[b_cat, bw], writes=[b_pac[pi]])
                    sc.add("dve", (lambda s_=s_, pi=pi, csl=csl: lambda e: e.tensor_tensor(out=xs[:, s_, csl], in0=pac[:, pi, :], in1=xs[:, s_, csl], op=ALU.add))(),
                           reads=[b_pac[pi]], writes=[b_xs[s_]])
            if tk + 1 < ntk:
                load_concat(tk + 1)
            norm_transpose(gffn)
            for pn in range(11):
                csl = slice(pn * 512, (pn + 1) * 512)
                Wg, bwg = load_panel(wgv[:, :, csl], 16)
                Wu, bwu = load_panel(wuv[:, :, csl], 16)
                for jj in range(4):
                    j = pn * 4 + jj
                    pa = next_pac()
                    pb = next_pac()
                    r = cnt[0] % 2
                    cnt[0] += 1

                    def mmg(e, W=Wg, jj=jj, pi=pa):
                        ins = None
                        for kc in range(NKC):
                            ins = e.matmul(pac[:, pi, :], lhsT=W[:, kc, jj * 128:(jj + 1) * 128], rhs=hT[:, kc, :], start=(kc == 0), stop=(kc == NKC - 1))
                        return ins

                    def mmu(e, W=Wu, jj=jj, pi=pb):
                        ins = None
                        for kc in range(NKC):
                            ins = e.matmul(pac[:, pi, :], lhsT=W[:, kc, jj * 128:(jj + 1) * 128], rhs=hT[:, kc, :], start=(kc == 0), stop=(kc == NKC - 1))
                        return ins
                    sc.add("pe", mmg, reads=[b_hT, bwg], writes=[b_pac[pa]])
                    sc.add("pe", mmu, reads=[b_hT, bwu], writes=[b_pac[pb]])
                    sc.add("act", (lambda r=r, pa=pa: lambda e: e.activation(out=sg[:, r, :], in_=pac[:, pa, :], func=AF.Silu))(),
                           reads=[b_pac[pa]], writes=[b_sg[r]])
                    wl = [b_actT[j], (b_gt[j // 8] if j < 32 else b_et[(j - 32) // 8])]
                    sc.add("dve", (lambda r=r, pb=pb, j=j: lambda e: e.tensor_tensor(out=actT[:, j, :], in0=pac[:, pb, :], in1=sg[:, r, :], op=ALU.mult))(),
                           reads=[b_pac[pb], b_sg[r]], writes=wl)
            for nch in range(4):
                csl = slice(nch * 512, (nch + 1) * 512)
                for (c0, c1) in ((0, 16), (16, 32), (32, 44)):
                    Wd, bwd = load_panel(wdv[:, c0:c1, csl], c1 - c0)
                    for s_ in range(4):
                        def mmd(e, s_=s_, c0=c0, c1=c1, Wd=Wd):
                            ins = None
                            for j in range(c0, c1):
                                ins = e.matmul(pac[:, s_, :], lhsT=actT[:, j, s_ * 128:(s_ + 1) * 128], rhs=Wd[:, j - c0, :], start=(j == 0), stop=(j == 43))
                            return ins
                        sc.add("pe", mmd, reads=[bwd, b_fU] + b_actT[c0:c1], writes=[b_pac[s_]])
                for s_ in range(4):
                    sc.add("dve", (lambda s_=s_, csl=csl: lambda e: e.tensor_tensor(out=xs[:, s_, csl], in0=pac[:, s_, :], in1=xs[:, s_, csl], op=ALU.add))(),
                           reads=[b_pac[s_]], writes=[b_xs[s_]])
            pacn[0] = 4
            norm_transpose(gple)
            for s_ in range(4):
                r = cnt[0] % 2
                cnt[0] += 1
                r0 = tk * 512 + s_ * 128
                sc.add("sp", (lambda r=r, r0=r0: lambda e: e.dma_start(out=pt[:, r, :], in_=po[r0:r0 + 128, :]))(), writes=[b_pt[r]], dma=True)
                sc.add("dve", (lambda r=r: lambda e: e.tensor_copy(out=ptb[:, r, :], in_=pt[:, r, :]))(), reads=[b_pt[r]], writes=[b_ptb[r]])

                def trp(e, r=r):
                    e.transpose(out=ptr[:, 0, 0:128], in_=ptb[:, r, 0:128], identity=ident_bf[:])
                    return e.transpose(out=ptr[:, 0, 128:256], in_=ptb[:, r, 128:256], identity=ident_bf[:])
                sc.add("pe", trp, reads=[b_ptb[r], b_ident], writes=[b_ptr[0]])
                sc.add("dve", (lambda s_=s_: lambda e: e.tensor_copy(out=ppT[:, :, s_ * 128:(s_ + 1) * 128], in_=ptr[:, 0, 0:256].rearrange("p (c t) -> p c t", c=2)))(),
                       reads=[b_ptr[0]], writes=[b_ppT])
            for nch in range(4):
                csl = slice(nch * 512, (nch + 1) * 512)
                Wp, bw = load_panel(wpgv[:, :, csl], 16)
                rw = nch % 2
                sc.add("sp", (lambda rw=rw, csl=csl: lambda e: e.dma_start(out=wppT[:, rw], in_=wppv[:, :, csl]))(), reads=[conv_done], writes=[b_wpp[rw]], dma=True)
                for s_ in range(4):
                    pa = next_pac()
                    pb = next_pac()

                    def mmq(e, s_=s_, pi=pa, Wp=Wp):
                        ins = None
                        for kc in range(NKC):
                            ins = e.matmul(pac[:, pi, :], lhsT=hT[:, kc, s_ * 128:(s_ + 1) * 128], rhs=Wp[:, kc, :], start=(kc == 0), stop=(kc == NKC - 1))
                        return ins

                    def mme(e, s_=s_, pi=pb, rw=rw):
                        e.matmul(pac[:, pi, :], lhsT=ppT[:, 0, s_ * 128:(s_ + 1) * 128], rhs=wppT[:, rw, 0, :], start=True, stop=False)
                        return e.matmul(pac[:, pi, :], lhsT=ppT[:, 1, s_ * 128:(s_ + 1) * 128], rhs=wppT[:, rw, 1, :], start=False, stop=True)
                    sc.add("pe", mmq, reads=[b_hT, bw], writes=[b_pac[pa]])
                    sc.add("pe", mme, reads=[b_ppT, b_wpp[rw]], writes=[b_pac[pb]])
                    sc.add("act", (lambda s_=s_, pa=pa, csl=csl: lambda e: e.activation(out=gt[:, s_, csl], in_=pac[:, pa, :], func=AF.Sigmoid))(),
                           reads=[b_pac[pa]], writes=[b_gt[s_], b_fU])
                    sc.add("dve", (lambda s_=s_, pb=pb, csl=csl: lambda e: e.tensor_copy(out=et[:, s_, csl], in_=pac[:, pb, :]))(),
                           reads=[b_pac[pb]], writes=[b_et[s_], b_fU])
            if tk + 1 < ntk:
                for nch in range(2):
                    wout_pref.append(load_panel(woutv[:, :, nch * 512:(nch + 1) * 512], 16))
            for s_ in range(4):
                r = cnt[0] % 2
                cnt[0] += 1
                r0 = tk * 512 + s_ * 128
                sc.add("act", (lambda r=r, s_=s_: lambda e: e.activation(out=sqj[:], in_=et[:, s_, :], func=AF.Square, accum_out=stat[:, 4 * r:4 * r + 1]))(),
                       reads=[b_et[s_]], writes=[b_sg[0], b_sg[1], b_stat[r]])
                sc.add("act", (lambda r=r: lambda e: e.activation(out=stat[:, 4 * r + 1:4 * r + 2], in_=stat[:, 4 * r:4 * r + 1], func=AF.Sqrt, bias=epsb[:], scale=1.0 / D))(),
                       reads=[b_small], writes=[b_stat[r]])
                sc.add("dve", (lambda r=r: lambda e: e.reciprocal(out=stat[:, 4 * r + 2:4 * r + 3], in_=stat[:, 4 * r + 1:4 * r + 2]))(), writes=[b_stat[r]])
                sc.add("dve", (lambda r=r, s_=s_: lambda e: e.scalar_tensor_tensor(out=et[:, s_, :], in0=et[:, s_, :], scalar=stat[:, 4 * r + 2:4 * r + 3], in1=wpb[:, 0, :],
                                                                                  op0=ALU.mult, op1=ALU.mult))(),
                       reads=[b_stat[r], b_small], writes=[b_et[s_]])
                sc.add("dve", (lambda s_=s_: lambda e: e.tensor_tensor(out=et[:, s_, :], in0=et[:, s_, :], in1=gt[:, s_, :], op=ALU.mult))(),
                       reads=[b_gt[s_]], writes=[b_et[s_]])
                sc.add("dve", (lambda s_=s_: lambda e: e.tensor_tensor(out=xs[:, s_, :], in0=xs[:, s_, :], in1=et[:, s_, :], op=ALU.add))(),
                       reads=[b_et[s_]], writes=[b_xs[s_]])
                sc.add("sp", (lambda s_=s_, r0=r0: lambda e: e.dma_start(out=y[r0:r0 + 128, :], in_=xs[:, s_, :]))(), reads=[b_xs[s_]], dma=True)
        run_block()


def _act_evac8(e, hT, ptr, gmix, half, pb, s_):
    ins = None
    for j in range(8):
        kc = half * 8 + j
        ins = e.activation(out=hT[:, kc, s_ * 128:(s_ + 1) * 128], in_=ptr[:, pb, j * 128:(j + 1) * 128],
                           func=AF.Copy, scale=gmix[:, kc:kc + 1])
    return ins


def _finish(nc, sc, es, y, xo, run_block):
    t = es.enter_context(nc.sbuf_tensor("dbg_t", [128, 2048], F32))
    b = Buf()
    for i in range(y.shape[0] // 128):
        sc.add("sp", (lambda i=i: lambda e: e.dma_start(out=t[:], in_=xo[i * 128:(i + 1) * 128, :]))(), writes=[b], dma=True)
        sc.add("sp", (lambda i=i: lambda e: e.dma_start(out=y[i * 128:(i + 1) * 128, :], in_=t[:]))(), reads=[b], dma=True)
    run_block()


def prep_inputs(inp):
    f = lambda a: np.ascontiguousarray(np.asarray(a, dtype=np.float32))
    x = f(inp["x"]); p = f(inp["p"])[0]
    w_in = f(inp["w_in"])[0]
    t16 = lambda v: np.ascontiguousarray(f(v)[0].reshape(16, 128).T)
    perm = np.concatenate([np.arange(0, 512), np.arange(1024, 1536), np.arange(512, 1024), np.arange(1536, 2048)])
    shared = {
        "wnmix": t16(inp["w_norm_mix"]),
        "qkg": np.ascontiguousarray(np.stack([f(inp["q_norm_w"])[0], f(inp["k_norm_w"])[0]], axis=1)),
        "wout": np.ascontiguousarray(f(inp["w_out"])[0][perm]),
        "wnffn": t16(inp["w_norm_ffn"]),
        "wg": f(inp["w_ffn_gate"])[0], "wu": f(inp["w_ffn_up"])[0], "wd": f(inp["w_ffn_down"])[0],
        "wnple": t16(inp["w_norm_ple"]),
        "wpg": f(inp["w_ple_gate"])[0], "wpp": f(inp["w_ple_proj"])[0],
        "wpost": f(inp["w_ple_post_norm"])[0].reshape(1, 2048),
    }
    cw = f(inp["mlstm_conv_w"])[0]
    cb = f(inp["mlstm_conv_b"])[0]
    maps = []
    for c in range(8):
        b, g = divmod(c, 2)
        cols = np.concatenate([
            np.arange(512 * g, 512 * g + 512),
            1024 + np.arange(512 * g, 512 * g + 512),
            3080 + np.arange(256 * g, 256 * g + 256),
            3592 + np.arange(256 * g, 256 * g + 256),
            5136 + np.arange(512 * g, 512 * g + 512),
            2048 + np.arange(512 * g, 512 * g + 512),
            4104 + np.arange(512 * g, 512 * g + 512),
            3072 + np.arange(4 * g, 4 * g + 4),
            5128 + np.arange(2 * g, 2 * g + 2),
            5132 + np.arange(2 * g, 2 * g + 2),
        ])
        ch = np.concatenate([np.arange(256 * g, 256 * g + 256), 512 + np.arange(256 * g, 256 * g + 256)])
        m = dict(shared)
        m["xb"] = x[b]
        m["xo"] = np.ascontiguousarray(x[b, 2048 * g:2048 * g + 2048])
        m["po"] = np.ascontiguousarray(p[b, 2048 * g:2048 * g + 2048])
        m["w1"] = np.ascontiguousarray(w_in[:, cols])
        m["bias8"] = np.concatenate([f(inp["fox_f_bias"])[0, 4 * g:4 * g + 4], f(inp["mlstm_i_bias"])[0, 2 * g:2 * g + 2],
                                     f(inp["mlstm_f_bias"])[0, 2 * g:2 * g + 2]]).reshape(8, 1).astype(np.float32)
        m["convw"] = np.ascontiguousarray(cw[:, ch].reshape(4, 4, 128).transpose(2, 1, 0))
        m["convb"] = np.ascontiguousarray(cb[ch].reshape(4, 128).T)
        m["onw"] = np.ascontiguousarray(f(inp["mlstm_out_norm_w"])[0, 512 * g:512 * g + 512].reshape(4, 128).T)
        maps.append(m)
    return maps


_NC_CACHE = {}


def kernel(**inputs):
    maps = prep_inputs(inputs)
    if "nc" not in _NC_CACHE:
        _NC_CACHE["nc"] = build_nc()
    nc = _NC_CACHE["nc"]
    res = run_bass_kernel_spmd(nc, maps, core_ids=list(range(8)))
    out = np.empty((4, S, D), dtype=np.float32)
    for c in range(8):
        b, g = divmod(c, 2)
        out[b, 2048 * g:2048 * g + 2048] = np.asarray(res.results[c]["y"], dtype=np.float32)
    return out
```

```python
import numpy as np
import concourse.bass as bass
import concourse.mybir as mybir
from concourse.bass_utils import run_bass_kernel_spmd

F32 = mybir.dt.float32
BF16 = mybir.dt.bfloat16
AF = mybir.ActivationFunctionType
ALU = mybir.AluOpType
AX = mybir.AxisListType

S = 4096
D = 2048
DFF = 5632
NKC = 16
W1 = 3080
EPS = 1e-6
NEG = -30000.0
DEBUG = None


class Buf:
    __slots__ = ("w", "r", "name")

    def __init__(self, name=""):
        self.w = None
        self.r = []
        self.name = name


class Op:
    __slots__ = ("eng", "idx", "fn", "deps", "dma", "dsem", "dval", "prev_dma", "cc")


NDSEM = 12
COMPUTE = ("pe", "act", "dve", "pool")


class Sched:
    def __init__(self, nc, sems):
        self.nc = nc
        self.sems = sems
        self.q = {e: [] for e in ("pe", "act", "dve", "pool", "sp")}
        self.count = {e: 0 for e in COMPUTE}
        self.ndma = {"sp": 0, "pool": 0, "act": 0}
        self.dma_ops = {"sp": [], "pool": [], "act": []}
        self.waited = {e: {} for e in self.q}
        self.cc_ops = []

    def add(self, eng, fn, reads=(), writes=(), dma=False):
        op = Op()
        op.eng = eng
        op.fn = fn
        op.dma = dma
        op.cc = False
        deps = []
        for b in reads:
            if b.w is not None:
                deps.append(b.w)
        for b in writes:
            if b.w is not None:
                deps.append(b.w)
            deps.extend(b.r)
        for b in reads:
            b.r.append(op)
        for b in writes:
            b.w = op
            b.r = []
        op.deps = [d for d in deps if d is not op]
        if dma:
            n = self.ndma[eng]
            op.dsem = self.sems[("d", eng, n % NDSEM)]
            op.dval = 16 * (n // NDSEM + 1)
            lag = 1 if eng == "pool" else NDSEM
            op.prev_dma = self.dma_ops[eng][n - lag] if n >= lag else None
            self.ndma[eng] = n + 1
            self.dma_ops[eng].append(op)
            op.idx = None
        else:
            self.count[eng] += 1
            op.idx = self.count[eng]
        self.q[eng].append(op)
        return op

    def _wait_for(self, eng, handle, dep):
        if dep.dma:
            sem, val = dep.dsem, dep.dval
        else:
            if dep.eng == "pe" and eng == "pe":
                return
            sem, val = self.sems[dep.eng], dep.idx
        key = id(sem)
        if self.waited[eng].get(key, 0) >= val:
            return
        self.waited[eng][key] = val
        handle.wait_ge(sem, val)

    def _emit_one(self, eng, h, final):
        for op in self.q[eng]:
            for d in op.deps:
                self._wait_for(eng, h, d)
            if op.dma and op.prev_dma is not None:
                self._wait_for(eng, h, op.prev_dma)
            ins = op.fn(h)
            if op.cc:
                ins.then_inc(op.dsem)
            elif op.dma:
                ins.then_inc(op.dsem, 16)
            else:
                ins.then_inc(self.sems[eng], 1)
        self.q[eng] = []
        for e2 in COMPUTE:
            if e2 == eng or final[e2] == 0:
                continue
            sem = self.sems[e2]
            if self.waited[eng].get(id(sem), 0) < final[e2]:
                self.waited[eng][id(sem)] = final[e2]
                h.wait_ge(sem, final[e2])
        for qn, lst in self.dma_ops.items():
            for op in lst[-NDSEM:]:
                self._wait_for(eng, h, op)

    def run_block(self):
        final = {e: self.count[e] for e in COMPUTE}
        with self.nc.Block() as block:
            @block.tensor
            def _(e):
                self._emit_one("pe", e, final)

            @block.scalar
            def _(e):
                self._emit_one("act", e, final)

            @block.vector
            def _(e):
                self._emit_one("dve", e, final)

            @block.gpsimd
            def _(e):
                self._emit_one("pool", e, final)

            @block.sync
            def _(e):
                self._emit_one("sp", e, final)


_REG = {}


def _fillreg(e, val):
    if val not in _REG:
        _REG[val] = e.to_reg(val)
    return _REG[val]


def build_nc(debug=None):
    _REG.clear()
    nc = bass.Bass("TRN2", target_bir_lowering=False)

    lite = debug is not None and debug.get("lite")

    def din(name, shape, dt=F32):
        if lite and name in ("wout", "wg", "wu", "wd", "wpg", "wpp"):
            shape = [128, 128]
        if lite and name == "xb":
            shape = [debug.get("ntt", 8) * 512, D]
        if lite and name in ("xo", "po"):
            shape = [128, shape[1]]
        return nc.dram_tensor(name, list(shape), dt, kind="ExternalInput").ap()

    xb = din("xb", [S, D])
    xo = din("xo", [2048, D])
    po = din("po", [2048, 256])
    w1 = din("w1", [D, W1])
    wnmix = din("wnmix", [128, NKC])
    bias8 = din("bias8", [8, 1])
    qkg = din("qkg", [128, 2])
    convw = din("convw", [128, 4, 4])
    convb = din("convb", [128, 4])
    onw = din("onw", [128, 4])
    wout = din("wout", [D, D])
    wnffn = din("wnffn", [128, NKC])
    wg = din("wg", [D, DFF])
    wu = din("wu", [D, DFF])
    wd = din("wd", [DFF, D])
    wnple = din("wnple", [128, NKC])
    wpg = din("wpg", [D, D])
    wpp = din("wpp", [256, D])
    wpost = din("wpost", [1, D])
    y = nc.dram_tensor("y", [128 if lite else 2048, D], F32, kind="ExternalOutput").ap()

    dbg = {}

    def dscr(name, shape, dt):
        if debug is not None and (name in debug or (name.startswith("bounce") and "bounce" in debug)):
            t = nc.dram_tensor(name, list(shape), dt, kind="ExternalOutput")
        else:
            t = nc.dram_tensor(name, list(shape), dt)
        return t

    qk_scr = dscr("qk_scr", [8, 128, S], BF16).ap()
    mqk_scr = dscr("mqk_scr", [4, 128, S], F32).ap()
    mo_scr = dscr("mo_scr", [4, 128, S], BF16).ap()
    v_scr = dscr("v_scr", [S, 1024], BF16).ap()
    g_scr = dscr("g_scr", [16, S], F32).ap()
    m_scr = dscr("m_scr", [1, 16], F32).ap()
    dbg_bounce = debug is not None and "bounce" in debug
    bounce_t = [dscr("bounce%d" % c, [128, S], BF16) for c in range(8)]
    gathered_t = [nc.dram_tensor("gathered%d" % c, [256, S], BF16) for c in range(8)]
    bounce = [t.ap() for t in bounce_t]

    names = {}
    with (
        nc.semaphore("s_pe") as s_pe, nc.semaphore("s_act") as s_act,
        nc.semaphore("s_dve") as s_dve, nc.semaphore("s_pool") as s_pool,
        nc.semaphore("s_cc") as s_cc,
    ):
        import contextlib
        es = contextlib.ExitStack()
        with es:
            sems = {"pe": s_pe, "act": s_act, "dve": s_dve, "pool": s_pool}
            for qn in ("sp", "pool", "act"):
                for j in range(NDSEM):
                    sems[("d", qn, j)] = es.enter_context(nc.semaphore(f"d_{qn}_{j}"))
            sc = Sched(nc, sems)

            run_block = sc.run_block

            ident_bf = es.enter_context(nc.sbuf_tensor("ident_bf", [128, 128], BF16))
            ident_f = es.enter_context(nc.sbuf_tensor("ident_f", [128, 128], F32))
            ones_bf = es.enter_context(nc.sbuf_tensor("ones_bf", [128, 128], BF16))
            b_ident = Buf()
            b_ones = Buf()

            for t_ in (ident_bf, ident_f):
                sc.add("pool", (lambda t_=t_: lambda e: e.memset(t_[:], 1.0))(), writes=[b_ident])
                sc.add("pool", (lambda t_=t_: lambda e: e.affine_select(out=t_[:], in_=t_[:], pattern=[[-1, 128]], compare_op=ALU.is_equal,
                                                                       fill=0.0, base=0, channel_multiplier=1))(), writes=[b_ident])
            sc.add("pool", lambda e: e.memset(ones_bf[:], 1.0), writes=[b_ones])

            ptr = es.enter_context(nc.psum_tensor("ptr", [128, 2, 1024], BF16))
            pac = es.enter_context(nc.psum_tensor("pac", [128, 6, 512], F32))
            b_ptr = [Buf(), Buf()]
            b_pac = [Buf() for _ in range(6)]

            gs = contextlib.ExitStack()
            gall = gs.enter_context(nc.sbuf_tensor("gall", [8, S], F32))
            b_gathered = [Buf() for _ in range(8)]
            b_gall = Buf()
            sc.add("pool", lambda e: e.memset(gall[:], 0.0), writes=[b_gall])

            wb = {}
            b_wb = {}
            conv_list = []
            for name, src, rows, cols in (("wout", wout, D, D), ("wg", wg, D, DFF), ("wu", wu, D, DFF), ("wd", wd, DFF, D),
                                          ("wpg", wpg, D, D), ("wpp", wpp, 256, D)):
                if lite:
                    continue
                wb[name] = nc.dram_tensor(name + "_b", [rows, cols], BF16).ap()
                b_wb[name] = Buf()
                sv = src.rearrange("(c p) n -> p c n", p=128)
                dv = wb[name].rearrange("(c p) n -> p c n", p=128)
                for c0 in range(0, cols, 512):
                    conv_list.append((name, sv[:, :, c0:c0 + 512], dv[:, :, c0:c0 + 512]))

            def emit_conv(k):
                for (name, s_ap, d_ap) in conv_list[:k]:
                    sc.add("pool", (lambda s_ap=s_ap, d_ap=d_ap: lambda e: e.dma_start(out=d_ap, in_=s_ap))(), reads=[b_wb[name]], dma=True)
                del conv_list[:k]
            only4 = debug is not None and debug.get("only4")
            ph = contextlib.ExitStack()
            with ph:
              if not only4:
                T = lambda name, shape, dt: ph.enter_context(nc.sbuf_tensor(name, shape, dt))
                w1s = T("w1s", [128, NKC, W1], BF16)
                xt = T("xt", [128, 2, D], F32)
                xn = T("xn", [128, 2, D], BF16)
                sqj = T("sqj", [128, D], BF16)
                hT2 = T("hT", [128, 2, NKC, 512], BF16)
                stat = T("stat", [128, 8], F32)
                gmix = T("gmix", [128, NKC], F32)
                qkgs = T("qkgs", [128, 2], F32)
                qraw = T("qraw", [128, 2, 512], F32)
                sq = T("sq", [128, 2, 512], BF16)
                rs = T("rs", [128, 2, 512], F32)
                rsc = T("rsc", [128, 2, 512], F32)
                qo = T("qo", [128, 2, 512], BF16)
                mqo = T("mqo", [128, 2, 512], F32)
                moo = T("moo", [128, 2, 512], BF16)
                vo = T("vo", [128, 2, 1024], BF16)
                epsb = T("epsb", [128, 1], F32)
                b_small = Buf()
                sc.add("pool", lambda e: e.memset(epsb[:], EPS), writes=[b_small])
                sc.add("sp", lambda e: e.dma_start(out=gmix[:], in_=wnmix), writes=[b_small], dma=True)
                sc.add("sp", lambda e: e.dma_start(out=qkgs[:], in_=qkg), writes=[b_small], dma=True)
                w1v = w1.rearrange("(c p) n -> p c n", p=128)
                b_w1p = [Buf() for _ in range(6)]
                b_w1 = [[Buf(), Buf()] for _ in range(4)]
                for pi in range(6 if not (debug or {}).get("skipw1") else 0):
                    sc.add("pool", (lambda pi=pi: lambda e: e.dma_start(out=w1s[:, :, pi * 512:(pi + 1) * 512], in_=w1v[:, :, pi * 512:(pi + 1) * 512]))(),
                           writes=[b_w1p[pi]], dma=True)
                gst = T("gst", [128, NKC, 8], F32)
                sc.add("sp", lambda e: e.dma_start(out=gst[:], in_=w1v[:, :, 3072:W1]), writes=[b_w1[3][0]], dma=True)
                sc.add("dve", lambda e: e.tensor_copy(out=w1s[:, :, 3072:W1], in_=gst[:]), reads=[b_w1[3][0]], writes=[b_w1[3][1]])
                b_xt = [Buf(), Buf()]
                b_xn = [Buf(), Buf()]
                b_sqj = Buf()
                b_hT = [Buf(), Buf()]
                b_stat = [Buf(), Buf()]
                b_qraw = [Buf(), Buf()]
                b_sq = [Buf(), Buf()]
                b_rs = [Buf(), Buf()]
                b_qo = [Buf(), Buf()]
                b_mqo = [Buf(), Buf()]
                b_moo = [Buf(), Buf()]
                b_vo = [Buf(), Buf()]
                pacn = [0]

                def next_pac():
                    i = pacn[0] % 6
                    pacn[0] += 1
                    return i

                xbv = xb.rearrange("(t p) d -> t p d", p=128)
                sub_global = [0]
                chunkctr = [0]
                NTT = 8 if debug is None else debug.get('ntt', 8)

                NSUBT = 4 * NTT
                stage_done = {"L": -1, "C": -1, "X": -1}

                def st_L(k):
                    r = k % 2
                    sc.add("sp", lambda e: e.dma_start(out=xt[:, r, :], in_=xbv[k]), writes=[b_xt[r]], dma=True)

                def st_C(k):
                    r = k % 2
                    sc.add("act", lambda e: e.activation(out=sqj[:], in_=xt[:, r, :], func=AF.Square, accum_out=stat[:, 4 * r:4 * r + 1]),
                           reads=[b_xt[r]], writes=[b_sqj, b_stat[r]])
                    sc.add("act", lambda e: e.activation(out=stat[:, 4 * r + 1:4 * r + 2], in_=stat[:, 4 * r:4 * r + 1], func=AF.Sqrt, bias=epsb[:], scale=1.0 / D),
                           reads=[b_small], writes=[b_stat[r]])
                    sc.add("dve", lambda e: e.reciprocal(out=stat[:, 4 * r + 2:4 * r + 3], in_=stat[:, 4 * r + 1:4 * r + 2]), writes=[b_stat[r]])
                    sc.add("dve", lambda e: e.tensor_scalar(out=xn[:, r, :], in0=xt[:, r, :], scalar1=stat[:, 4 * r + 2:4 * r + 3], scalar2=None, op0=ALU.mult),
                           reads=[b_xt[r], b_stat[r]], writes=[b_xn[r]])

                def st_X(k):
                    r = k % 2
                    tt_, s_ = divmod(k, 4)
                    hb = tt_ % 2
                    hT = hT2[:, hb]
                    for half in range(2):
                        def tr(e, half=half):
                            ins = None
                            for j in range(8):
                                kc = half * 8 + j
                                ins = e.transpose(out=ptr[:, half, j * 128:(j + 1) * 128], in_=xn[:, r, kc * 128:(kc + 1) * 128], identity=ident_bf[:])
                            return ins
                        sc.add("pe", tr, reads=[b_xn[r], b_ident], writes=[b_ptr[half]])
                        if half == 0:
                            sc.add("dve", lambda e: e.tensor_tensor(
                                out=hT[:, 0:8, s_ * 128:(s_ + 1) * 128], in0=ptr[:, 0, :].rearrange("p (j t) -> p j t", j=8),
                                in1=gmix[:, 0:8].unsqueeze(2).to_broadcast([128, 8, 128]), op=ALU.mult),
                                reads=[b_ptr[0], b_small], writes=[b_hT[hb]])
                        else:
                            sc.add("act", lambda e: _act_evac8(e, hT, ptr, gmix, 1, 1, s_), reads=[b_ptr[1], b_small], writes=[b_hT[hb]])

                def ens_L(j):
                    if j >= NSUBT or stage_done["L"] >= j:
                        return
                    ens_L(j - 1)
                    if j >= 2:
                        ens_C(j - 2)
                    st_L(j)
                    stage_done["L"] = j

                def ens_C(j):
                    if j >= NSUBT or stage_done["C"] >= j:
                        return
                    ens_C(j - 1)
                    ens_L(j)
                    if j >= 2:
                        ens_X(j - 2)
                    st_C(j)
                    stage_done["C"] = j

                def ens_X(j):
                    if j >= NSUBT or stage_done["X"] >= j:
                        return
                    ens_X(j - 1)
                    ens_C(j)
                    st_X(j)
                    stage_done["X"] = j

                def advance(kx):
                    ens_X(kx)
                    ens_C(kx + 1)
                    ens_L(kx + 2)

                advance(3)
                for tt in range(NTT):
                    hb = tt % 2
                    hT = hT2[:, hb]
                    tsl = slice(tt * 512, (tt + 1) * 512)
                    deferred = []

                    def flush_deferred(keep=1):
                        while len(deferred) > keep:
                            deferred.pop(0)()

                    parts = (debug or {}).get("parts", "fgt")
                    for n in ((debug or {}).get("frange", range(16)) if "f" in parts else []):
                        pi = next_pac()
                        piece = n // 8

                        def mm(e, n=n, pi=pi, hT=hT):
                            ins = None
                            for kc in range(NKC):
                                ins = e.matmul(pac[:, pi, :], lhsT=w1s[:, kc, n * 128:(n + 1) * 128], rhs=hT[:, kc, :],
                                               start=(kc == 0), stop=(kc == NKC - 1))
                            return ins
                        sc.add("pe", mm, reads=[b_hT[hb], b_w1p[n // 4]], writes=[b_pac[pi]])
                        flush_deferred()
                        if n % 4 == 3 and tt + 1 < NTT:
                            advance(4 * (tt + 1) + n // 4)
                        if n < 8:
                            r = chunkctr[0] % 2
                            chunkctr[0] += 1
                            gcol = 0 if n < 4 else 1
                            sc.add("dve", (lambda r=r, pi=pi: lambda e: e.tensor_copy(out=qraw[:, r, :], in_=pac[:, pi, :]))(),
                                   reads=[b_pac[pi]], writes=[b_qraw[r]])
                            sc.add("act", (lambda r=r: lambda e: e.activation(out=sq[:, r, :], in_=qraw[:, r, :], func=AF.Square))(),
                                   reads=[b_qraw[r]], writes=[b_sq[r]])

                            def post(n=n, r=r, gcol=gcol, tsl=tsl):
                                p2 = next_pac()
                                sc.add("pe", lambda e: e.matmul(pac[:, p2, :], lhsT=ones_bf[:], rhs=sq[:, r, :], start=True, stop=True),
                                       reads=[b_sq[r], b_ones], writes=[b_pac[p2]])
                                sc.add("act", lambda e: e.activation(out=rs[:, r, :], in_=pac[:, p2, :], func=AF.Sqrt, bias=epsb[:], scale=1.0 / 128),
                                       reads=[b_pac[p2], b_small], writes=[b_rs[r]])
                                sc.add("dve", lambda e: e.reciprocal(out=rsc[:, r, :], in_=rs[:, r, :]), reads=[b_rs[r]], writes=[b_rs[r]])
                                sc.add("dve", lambda e: e.scalar_tensor_tensor(out=qo[:, r, :], in0=qraw[:, r, :], scalar=qkgs[:, gcol:gcol + 1],
                                                                                in1=rsc[:, r, :], op0=ALU.mult, op1=ALU.mult),
                                       reads=[b_qraw[r], b_rs[r], b_small], writes=[b_qo[r]])
                                sc.add("sp", lambda e: e.dma_start(out=qk_scr[n, :, tsl], in_=qo[:, r, :]), reads=[b_qo[r]], dma=True)
                            deferred.append(post)
                        elif n < 12:
                            r = chunkctr[0] % 2
                            chunkctr[0] += 1
                            sc.add("act", (lambda r=r, pi=pi: lambda e: e.activation(out=mqo[:, r, :], in_=pac[:, pi, :], func=AF.Copy))(),
                                   reads=[b_pac[pi]], writes=[b_mqo[r]])
                            sc.add("sp", (lambda r=r, n=n, tsl=tsl: lambda e: e.dma_start(out=mqk_scr[n - 8, :, tsl], in_=mqo[:, r, :]))(),
                                   reads=[b_mqo[r]], dma=True)
                        else:
                            r = chunkctr[0] % 2
                            chunkctr[0] += 1
                            sc.add("act", (lambda r=r, pi=pi: lambda e: e.activation(out=moo[:, r, :], in_=pac[:, pi, :], func=AF.Sigmoid))(),
                                   reads=[b_pac[pi]], writes=[b_moo[r]])
                            sc.add("sp", (lambda r=r, n=n, tsl=tsl: lambda e: e.dma_start(out=mo_scr[n - 12, :, tsl], in_=moo[:, r, :]))(),
                                   reads=[b_moo[r]], dma=True)
                    pi = next_pac()
                    if "g" not in parts:
                        continue

                    def mmg(e, pi=pi, hT=hT):
                        ins = None
                        for kc in range(NKC):
                            ins = e.matmul(pac[0:8, pi, :], lhsT=w1s[:, kc, 3072:3080], rhs=hT[:, kc, :],
                                           start=(kc == 0), stop=(kc == NKC - 1))
                        return ins
                    sc.add("pe", mmg, reads=[b_hT[hb], b_w1[3][0], b_w1[3][1]], writes=[b_pac[pi]])
                    flush_deferred(0)
                    sc.add("dve", (lambda pi=pi, tsl=tsl: lambda e: e.tensor_copy(out=gall[:, tsl], in_=pac[0:8, pi, :]))(),
                           reads=[b_pac[pi]], writes=[b_gall])
                    for s_ in (range(4) if "t" in parts else []):
                        r = chunkctr[0] % 2
                        chunkctr[0] += 1
                        for hv in range(2):
                            pi = next_pac()

                            def mmv(e, s_=s_, hv=hv, pi=pi, hT=hT):
                                ins = None
                                for kc in range(NKC):
                                    ins = e.matmul(pac[:, pi, :], lhsT=hT[:, kc, s_ * 128:(s_ + 1) * 128],
                                                   rhs=w1s[:, kc, 2048 + hv * 512:2048 + (hv + 1) * 512],
                                                   start=(kc == 0), stop=(kc == NKC - 1))
                                return ins
                            sc.add("pe", mmv, reads=[b_hT[hb], b_w1p[4 + hv]], writes=[b_pac[pi]])
                            if hv == 0:
                                sc.add("act", (lambda r=r, pi=pi, hv=hv: lambda e: e.activation(out=vo[:, r, hv * 512:(hv + 1) * 512], in_=pac[:, pi, :], func=AF.Copy))(),
                                       reads=[b_pac[pi]], writes=[b_vo[r]])
                            else:
                                sc.add("dve", (lambda r=r, pi=pi, hv=hv: lambda e: e.tensor_copy(out=vo[:, r, hv * 512:(hv + 1) * 512], in_=pac[:, pi, :]))(),
                                       reads=[b_pac[pi]], writes=[b_vo[r]])
                        row0 = tt * 512 + s_ * 128
                        sc.add("sp", (lambda r=r, row0=row0: lambda e: e.dma_start(out=v_scr[row0:row0 + 128, :], in_=vo[:, r, :]))(),
                               reads=[b_vo[r]], dma=True)
                run_block()
            if debug is not None and debug.get("stop") == 1:
                gs.close()
                _finish(nc, sc, es, y, xo, run_block)
                return nc
            if not only4:
                _phase2(nc, sc, es, run_block, debug, dict(
                    gall=gall, b_gall=b_gall, ident_f=ident_f, ident_bf=ident_bf, ones_bf=ones_bf, b_ident=b_ident, b_ones=b_ones,
                    pac=pac, b_pac=b_pac, ptr=ptr, b_ptr=b_ptr, bias8=bias8, g_scr=g_scr, m_scr=m_scr, qk_scr=qk_scr,
                    mqk_scr=mqk_scr, mo_scr=mo_scr, v_scr=v_scr, convw=convw, convb=convb, onw=onw, bounce=bounce,
                    bounce_t=bounce_t, gathered_t=gathered_t, s_cc=s_cc, b_gathered=b_gathered, emit_conv=emit_conv, conv_list=conv_list))
            if only4:
                run_block()
            gs.close()
            if debug is not None and debug.get("stop") == 2:
                _finish(nc, sc, es, y, xo, run_block)
                return nc
            emit_conv(len(conv_list))
            dbg_cat = None
            if debug is not None and not debug.get("exchange", True):
                dbg_cat = nc.dram_tensor("dbg_cat", [D, 2048], BF16, kind="ExternalInput").ap()
            _phase4(nc, sc, es, run_block, debug, dict(
                ident_bf=ident_bf, ones_bf=ones_bf, b_ident=b_ident, b_ones=b_ones, pac=pac, b_pac=b_pac, ptr=ptr, b_ptr=b_ptr,
                gathered_t=gathered_t, xo=xo, po=po, wout=wout, wnffn=wnffn, wg=wg, wu=wu, wd=wd, wnple=wnple, wpg=wpg, wpp=wpp,
                wpost=wpost, y=y, cc_buf=b_gathered, dbg_cat=dbg_cat, wb=wb, b_wb=b_wb))
    return nc


def _phase2(nc, sc, es, run_block, debug, A):
    import contextlib
    gall = A["gall"]; pac = A["pac"]; b_pac = A["b_pac"]; g_scr = A["g_scr"]; m_scr = A["m_scr"]
    ident_f = A["ident_f"]; ones_bf = A["ones_bf"]; b_ident = A["b_ident"]; b_ones = A["b_ones"]
    qk_scr = A["qk_scr"]; mqk_scr = A["mqk_scr"]; mo_scr = A["mo_scr"]; v_scr = A["v_scr"]; bounce = A["bounce"]
    b_gall = A["b_gall"]
    nJ = 8 if debug is None else debug.get("nJ", 8)
    heads_a = range(4) if debug is None else range(debug.get("nha", 4))
    heads_m = range(2) if debug is None else range(debug.get("nhm", 2))
    pers = contextlib.ExitStack()
    with pers:
        P = lambda name, shape, dt: pers.enter_context(nc.sbuf_tensor(name, shape, dt))
        negc = P("negc", [128, 32, 4], F32)
        acol = P("acol", [128, 32, 2], F32)
        negMbc = P("negMbc", [128, 16], F32)
        mqkT = P("mqkT", [128, 4, S], BF16)
        epsb = P("epsb2", [128, 1], F32)
        c_pad = P("c_pad", [128, S], BF16)
        sel = P("sel", [128, 4, 128], BF16)
        mneg = P("mneg", [128, 4, 512], BF16)
        m01 = P("m01", [128, 4, 512], BF16)
        negMdk = P("negMdk", [128, 16], F32)
        b_cpad = Buf(); b_sel = Buf(); b_mneg = Buf(); b_m01 = Buf(); b_negMdk = Buf()
        b_negc = Buf(); b_acol = Buf(); b_negM = Buf(); b_eps = Buf()
        RS = 128.0 ** 0.5
        sc.add("pool", lambda e: e.memset(c_pad[:], 0.0), writes=[b_cpad])
        sc.add("pool", lambda e: e.memset(sel[:], 1.0), writes=[b_sel])
        sc.add("pool", lambda e: e.memset(mneg[:], 0.0), writes=[b_mneg])
        sc.add("pool", lambda e: e.memset(m01[:], 1.0), writes=[b_m01])
        for a_ in range(4):
            sc.add("pool", (lambda a_=a_: lambda e: e.affine_select(out=sel[:, a_, :], in_=sel[:, a_, :], pattern=[[0, 128]], compare_op=ALU.is_equal,
                                                                   fill=_fillreg(e, 0.0), base=-a_, channel_multiplier=1))(), writes=[b_sel])
            sc.add("pool", (lambda a_=a_: lambda e: e.affine_select(out=mneg[:, a_, :], in_=mneg[:, a_, :], pattern=[[1, 512]], compare_op=ALU.is_ge,
                                                                   fill=_fillreg(e, NEG * RS), base=-128 * a_, channel_multiplier=-1))(), writes=[b_mneg])
            sc.add("pool", (lambda a_=a_: lambda e: e.affine_select(out=m01[:, a_, :], in_=m01[:, a_, :], pattern=[[1, 512]], compare_op=ALU.is_ge,
                                                                   fill=_fillreg(e, 0.0), base=-128 * a_, channel_multiplier=-1))(), writes=[b_m01])
        b_mqkT = [Buf() for _ in range(4)]
        b_gscr = Buf(); b_mscr = Buf()
        sc.add("pool", lambda e: e.memset(epsb[:], EPS), writes=[b_eps])
        ph = contextlib.ExitStack()
        with ph:
            T = lambda name, shape, dt: ph.enter_context(nc.sbuf_tensor(name, shape, dt))
            b8 = T("b8", [8, 1], F32)
            one8 = T("one8", [8, 1], F32)
            e8 = T("e8", [8, S], F32)
            cs8 = T("cs8", [8, S], F32)
            z8 = T("z8", [8, S], F32)
            A2 = T("A2", [2, S], F32)
            F2 = T("F2", [2, S], F32)
            tmax = T("tmax", [2, 16], F32)
            ucol = T("ucol", [128, 32, 8], F32)
            ccol = T("ccol", [128, 32, 8], F32)
            b_b8 = Buf(); b_e8 = Buf(); b_cs8 = Buf(); b_z8 = Buf(); b_A2 = Buf(); b_F2 = Buf(); b_tm = Buf()
            b_ucol = Buf(); b_ccol = Buf()
            sc.add("sp", lambda e: e.dma_start(out=b8[:], in_=A["bias8"]), writes=[b_b8], dma=True)
            b_one8 = Buf()
            sc.add("pool", lambda e: e.memset(one8[:], 1.0), writes=[b_one8])
            sc.add("pool", lambda e: e.memset(z8[:], 0.0), writes=[b_z8])
            sc.add("dve", lambda e: e.tensor_scalar(out=gall[:], in0=gall[:], scalar1=b8[:], scalar2=None, op0=ALU.add),
                   reads=[b_b8], writes=[b_gall])
            sc.add("act", lambda e: e.activation(out=e8[:], in_=gall[:], func=AF.Exp, scale=-1.0), reads=[b_gall], writes=[b_e8])
            sc.add("act", lambda e: e.activation(out=e8[:], in_=e8[:], func=AF.Ln, bias=one8[:], scale=1.0), reads=[b_one8], writes=[b_e8])
            sc.add("dve", lambda e: e.tensor_scalar(out=e8[:], in0=e8[:], scalar1=-1.0, scalar2=None, op0=ALU.mult), writes=[b_e8])
            sc.add("dve", lambda e: e.tensor_tensor_scan(out=cs8[:], data0=e8[:], data1=z8[:], initial=0.0, op0=ALU.add, op1=ALU.add),
                   reads=[b_e8, b_z8], writes=[b_cs8])
            sc.add("dve", lambda e: e.tensor_scalar(out=c_pad[0:4, :], in0=cs8[0:4, :], scalar1=RS, scalar2=None, op0=ALU.mult), reads=[b_cs8], writes=[b_cpad])
            sc.add("sp", lambda e: e.dma_start(out=g_scr[0:8, :], in_=gall[:]), reads=[b_gall], writes=[b_gscr], dma=True)
            sc.add("sp", lambda e: e.dma_start(out=g_scr[8:16, :], in_=cs8[:]), reads=[b_cs8], writes=[b_gscr], dma=True)
            for src, bsrc, dst, bdst, bank in ((gall, b_gall, ucol, b_ucol, 0), (cs8, b_cs8, ccol, b_ccol, 1)):
                def trg(e, src=src, bank=bank):
                    ins = None
                    for blk in range(32):
                        ins = e.transpose(out=pac[:, bank, blk * 8:(blk + 1) * 8], in_=src[:, blk * 128:(blk + 1) * 128],
                                          identity=ident_f[0:8, 0:8])
                    return ins
                sc.add("pe", trg, reads=[bsrc, b_ident], writes=[b_pac[bank]])
                sc.add("dve", (lambda dst=dst, bank=bank: lambda e: e.tensor_copy(out=dst[:].rearrange("p b g -> p (b g)"), in_=pac[:, bank, 0:256]))(),
                       reads=[b_pac[bank]], writes=[bdst])
            sc.add("dve", lambda e: e.tensor_scalar(out=negc[:], in0=ccol[:, :, 0:4], scalar1=-1.0, scalar2=None, op0=ALU.mult),
                   reads=[b_ccol], writes=[b_negc])
            sc.add("dve", lambda e: e.tensor_tensor(out=acol[:], in0=ucol[:, :, 4:6], in1=ccol[:, :, 6:8], op=ALU.subtract),
                   reads=[b_ucol, b_ccol], writes=[b_acol])
            sc.add("sp", lambda e: e.dma_start(out=A2[:], in_=g_scr[4:6, :]), reads=[b_gscr], writes=[b_A2], dma=True)
            sc.add("sp", lambda e: e.dma_start(out=F2[:], in_=g_scr[14:16, :]), reads=[b_gscr], writes=[b_F2], dma=True)
            sc.add("dve", lambda e: e.tensor_tensor(out=A2[:], in0=A2[:], in1=F2[:], op=ALU.subtract), reads=[b_F2], writes=[b_A2])
            sc.add("dve", lambda e: e.tensor_reduce(out=tmax[:, 0:8], in_=A2[:].rearrange("p (j t) -> p j t", j=8), axis=AX.X, op=ALU.max),
                   reads=[b_A2], writes=[b_tm])
            sc.add("dve", lambda e: e.tensor_tensor_scan(out=tmax[:, 8:16], data0=tmax[:, 0:8], data1=tmax[:, 0:8], initial=-1e30,
                                                          op0=ALU.max, op1=ALU.max), writes=[b_tm])
            sc.add("dve", lambda e: e.tensor_scalar(out=tmax[:, 0:8], in0=tmax[:, 8:16], scalar1=-1.0, scalar2=None, op0=ALU.mult), writes=[b_tm])
            sc.add("sp", lambda e: e.dma_start(out=m_scr.rearrange("o (h j) -> (o h) j", h=2), in_=tmax[:, 0:8]), reads=[b_tm], writes=[b_mscr], dma=True)
            sc.add("sp", lambda e: e.dma_start(out=negMbc[:], in_=m_scr[0:1, :].partition_broadcast(128).rearrange("p o n -> p (o n)")),
                   reads=[b_mscr], writes=[b_negM], dma=True)
            sc.add("dve", lambda e: e.tensor_scalar(out=negMdk[:], in0=negMbc[:], scalar1=float(np.log(128.0 ** -0.5)), scalar2=None, op0=ALU.add), reads=[b_negM], writes=[b_negMdk])
            run_block()
        A["emit_conv"](len(A["conv_list"]))
        ph = contextlib.ExitStack()
        with ph:
            T = lambda name, shape, dt: ph.enter_context(nc.sbuf_tensor(name, shape, dt))
            qT = T("qT", [128, S], BF16)
            kT = T("kT", [128, S], BF16)
            Vh = T("Vh", [128, 32, 256], BF16)
            Cbc = T("Cbc", [128, 1, S], F32)
            moT = T("moT", [128, 2, S], BF16)
            NR = 5
            LA = 3
            raw = T("raw", [128, 2, S], F32)
            acc = T("acc", [128, S], F32)
            cw = T("cw", [128, 4, 4], F32)
            cb = T("cb", [128, 4], F32)
            b_raw = [Buf(), Buf()]; b_acc = Buf(); b_cw = Buf()
            sc.add("sp", lambda e: e.dma_start(out=cw[:], in_=A["convw"]), writes=[b_cw], dma=True)
            sc.add("sp", lambda e: e.dma_start(out=cb[:], in_=A["convb"]), writes=[b_cw], dma=True)
            conv_ops = []
            for ch in range(4):
                r_ = ch % 2
                conv_ops.append((lambda ch=ch, r_=r_: sc.add("sp", lambda e: e.dma_start(out=raw[:, r_, :], in_=mqk_scr[ch]), writes=[b_raw[r_]], dma=True)))
                conv_ops.append((lambda ch=ch, r_=r_: sc.add("dve", lambda e: e.tensor_scalar(out=acc[:], in0=raw[:, r_, :], scalar1=cw[:, ch, 3:4], scalar2=None, op0=ALU.mult),
                                                           reads=[b_raw[r_], b_cw], writes=[b_acc])))
                for sh in (1, 2, 3):
                    conv_ops.append((lambda ch=ch, r_=r_, sh=sh: sc.add("dve", lambda e: e.scalar_tensor_tensor(
                        out=acc[:, sh:], in0=raw[:, r_, 0:S - sh], scalar=cw[:, ch, 3 - sh:4 - sh], in1=acc[:, sh:], op0=ALU.mult, op1=ALU.add),
                        reads=[b_raw[r_]], writes=[b_acc])))
                conv_ops.append((lambda ch=ch: sc.add("act", lambda e: e.activation(out=mqkT[:, ch, :], in_=acc[:], func=AF.Silu, bias=cb[:, ch:ch + 1], scale=1.0),
                                                      reads=[b_acc, b_cw], writes=[b_mqkT[ch]])))
            conv_ops.pop(0)()
            pT = T("pT", [128, NR, 512], BF16)
            rz = T("rz", [128, 512], F32)
            rsc2 = T("rsc2", [128, 512], F32)
            rn2 = T("rn2", [128, 512], F32)
            ao = T("ao", [128, 2, 512], BF16)
            wc = T("wc", [128, 2, 32], F32)
            bb = T("bb", [128, 512], F32)
            hb = T("hb", [128, 2, 512], F32)
            sqh = T("sqh", [128, 2, 512], BF16)
            rn = T("rn", [128, 512], F32)
            onws = T("onws", [128, 4], F32)
            b_qT = Buf(); b_kT = Buf(); b_Vh = Buf(); b_Cbc = Buf(); b_moT = Buf()
            b_qT0 = Buf(); b_kT0 = Buf(); b_Vh0 = Buf()
            b_tmp = [Buf() for _ in range(NR)]; b_pT = [Buf() for _ in range(NR)]
            SB = [pac[:, 0, :], pac[:, 1, :], A["ptr"][:, 0, :].bitcast(F32), A["ptr"][:, 1, :].bitcast(F32), pac[:, 2, :], pac[:, 3, :]]
            b_SB = [b_pac[0], b_pac[1], A["b_ptr"][0], A["b_ptr"][1], b_pac[2], b_pac[3]]
            b_rz = Buf(); b_ao = [Buf(), Buf()]; b_wc = [Buf(), Buf()]; b_bb = Buf(); b_hb = Buf(); b_sqh = Buf(); b_rn = Buf()
            b_onw = Buf(); b_bounce = [Buf() for _ in range(8)]
            sc.add("sp", lambda e: e.dma_start(out=onws[:], in_=A["onw"]), writes=[b_onw], dma=True)
            do_exch = debug is None or debug.get("exchange", True)

            def exchange(c):
                if not do_exch:
                    return
                op = sc.add("pool", lambda e: e.collective_compute("AllGather", ALU.bypass, replica_groups=[[0, 1], [2, 3], [4, 5], [6, 7]],
                                                                   ins=[A["bounce_t"][c].ap().opt()], outs=[A["gathered_t"][c].ap().opt()]),
                            reads=[b_bounce[c]], writes=[A["b_gathered"][c]], dma=True)
                sc.ndma["pool"] -= 1
                sc.dma_ops["pool"].pop()
                sc.cc_ops.append(op)
                op.dsem = A["s_cc"]; op.dval = len(sc.cc_ops); op.prev_dma = None; op.cc = True
            scale = 128.0 ** -0.5
            rot = [0]
            aoc = [0]
            def do_attn(h):
                vsrc = v_scr[:, h * 128:(h + 1) * 128].rearrange("(b p) d -> p b d", p=128)
                sc.add("sp", (lambda h=h: lambda e: e.dma_start(out=qT[:, 0:512], in_=qk_scr[h][:, 0:512]))(), writes=[b_qT0], dma=True)
                sc.add("sp", (lambda h=h: lambda e: e.dma_start(out=kT[:, 0:512], in_=qk_scr[4 + h][:, 0:512]))(), writes=[b_kT0], dma=True)
                sc.add("sp", (lambda vsrc=vsrc: lambda e: e.dma_start(out=Vh[:, 0:4, 0:128], in_=vsrc[:, 0:4, :]))(), writes=[b_Vh0], dma=True)
                sc.add("sp", (lambda h=h: lambda e: e.dma_start(out=qT[:, 512:S], in_=qk_scr[h][:, 512:S]))(), writes=[b_qT], dma=True)
                sc.add("sp", (lambda h=h: lambda e: e.dma_start(out=kT[:, 512:S], in_=qk_scr[4 + h][:, 512:S]))(), writes=[b_kT], dma=True)
                sc.add("sp", (lambda vsrc=vsrc: lambda e: e.dma_start(out=Vh[:, 4:32, 0:128], in_=vsrc[:, 4:32, :]))(), writes=[b_Vh], dma=True)
                tiles = [(J, i) for J in range(nJ) for i in range(4 * J + 4)]
                pend = None

                def issue_S(J, i):
                    sb = rot[0] % 6
                    r = rot[0] % NR
                    rot[0] += 1
                    a_ = i - 4 * J

                    def smm(e, h=h):
                        e.matmul(SB[sb], lhsT=kT[:, i * 128:(i + 1) * 128], rhs=qT[:, J * 512:(J + 1) * 512], start=True, stop=False)
                        ins = e.matmul(SB[sb], lhsT=sel[:, h, :], rhs=c_pad[:, J * 512:(J + 1) * 512], start=False, stop=(a_ < 0))
                        if a_ >= 0:
                            ins = e.matmul(SB[sb], lhsT=A["ident_bf"][:], rhs=mneg[:, a_, :], start=False, stop=True)
                        return ins
                    sc.add("pe", smm, reads=[b_qT0 if J == 0 else b_qT, b_kT0 if i < 4 else b_kT, b_cpad, b_sel, b_mneg, b_ident], writes=[b_SB[sb]])
                    return sb, r

                def rest(J, i, sb, r, h=h):
                    ob = 4
                    zb = 5
                    last = 4 * J + 3
                    sc.add("act", lambda e: e.activation(out=pT[:, r, :], in_=SB[sb], func=AF.Exp, bias=negc[:, i, h:h + 1], scale=scale),
                           reads=[b_SB[sb], b_negc], writes=[b_pT[r]])

                    def pv(e):
                        e.matmul(pac[:, ob, :], lhsT=Vh[:, i, 0:128], rhs=pT[:, r, :], start=(i == 0), stop=(i == last))
                        return e.matmul(pac[:, zb, :], lhsT=ones_bf[:], rhs=pT[:, r, :], start=(i == 0), stop=(i == last))
                    sc.add("pe", pv, reads=[b_pT[r], b_Vh0 if i < 4 else b_Vh, b_ones], writes=[b_pac[ob], b_pac[zb]])
                    if i == last:
                        a = aoc[0] % 2
                        aoc[0] += 1
                        sc.add("act", lambda e: e.activation(out=hb[:, 0, :], in_=pac[:, ob, :], func=AF.Copy), reads=[b_pac[ob]], writes=[b_hb])
                        sc.add("dve", lambda e: e.tensor_copy(out=rsc2[:], in_=pac[:, zb, :]), reads=[b_pac[zb]], writes=[b_rz])
                        sc.add("dve", lambda e: e.reciprocal(out=rz[:], in_=rsc2[:]), writes=[b_rz])
                        sc.add("dve", lambda e: e.tensor_tensor(out=ao[:, a, :], in0=hb[:, 0, :], in1=rz[:], op=ALU.mult),
                               reads=[b_hb, b_rz], writes=[b_ao[a]])
                        sc.add("sp", lambda e: e.dma_start(out=bounce[h][:, J * 512:(J + 1) * 512], in_=ao[:, a, :]),
                               reads=[b_ao[a]], writes=[b_bounce[h]], dma=True)
                        for _ in range(2 if J >= 2 else 0):
                            if conv_ops:
                                conv_ops.pop(0)()
                pendq = []
                for (J, i) in tiles:
                    pendq.append((J, i) + issue_S(J, i))
                    if len(pendq) > 5:
                        rest(*pendq.pop(0))
                while pendq:
                    rest(*pendq.pop(0))
                exchange(h)
            dk = 128.0 ** -0.5

            def do_mlstm(hp):
                while conv_ops:
                    conv_ops.pop(0)()
                vsrc2 = v_scr[:, 512 + hp * 256:512 + (hp + 1) * 256].rearrange("(b p) d -> p b d", p=128)
                sc.add("sp", (lambda vsrc2=vsrc2: lambda e: e.dma_start(out=Vh[:, 0:4, :], in_=vsrc2[:, 0:4, :]))(), writes=[b_Vh0], dma=True)
                sc.add("sp", (lambda vsrc2=vsrc2: lambda e: e.dma_start(out=Vh[:, 4:32, :], in_=vsrc2[:, 4:32, :]))(), writes=[b_Vh], dma=True)
                sc.add("sp", (lambda hp=hp: lambda e: e.dma_start(out=moT[:], in_=mo_scr[2 * hp:2 * hp + 2].rearrange("c p s -> p c s")))(),
                       writes=[b_moT], dma=True)
                sc.add("sp", (lambda hp=hp: lambda e: e.dma_start(out=Cbc[:], in_=g_scr[14 + hp:15 + hp, :].partition_broadcast(128)))(),
                       reads=[b_gscr], writes=[b_Cbc], dma=True)
                qTm = mqkT[:, hp, :]
                kTm = mqkT[:, 2 + hp, :]
                tiles = [(J, i) for J in range(nJ) for i in range(4 * J + 4)]
                pend = None

                def issue_S2(J, i, hp=hp, qTm=qTm, kTm=kTm):
                    sb = rot[0] % 4
                    r = rot[0] % NR
                    rot[0] += 1
                    sc.add("pe", lambda e: e.matmul(SB[sb], lhsT=kTm[:, i * 128:(i + 1) * 128], rhs=qTm[:, J * 512:(J + 1) * 512], start=True, stop=True),
                           reads=[b_mqkT[hp], b_mqkT[2 + hp]], writes=[b_SB[sb]])
                    return sb, r

                tails = []
                tailsS = []
                tailsB = []

                def emit_wc(J, hp=hp):
                    w_ = J % 2
                    lastJ = 4 * J + 3
                    mc = hp * 8 + J
                    sc.add("act", lambda e: e.activation(out=wc[:, w_, 0:lastJ + 1], in_=acol[:, 0:lastJ + 1, hp], func=AF.Exp, bias=negMdk[:, mc:mc + 1], scale=1.0),
                           reads=[b_acol, b_negMdk], writes=[b_wc[w_]])

                def rest2(J, i, sb, r, hp=hp):
                    last = 4 * J + 3
                    w = J % 2
                    mcol = hp * 8 + J
                    if i < 4 * J:
                        sc.add("act", lambda e: e.activation(out=pT[:, r, :], in_=SB[sb], func=AF.Copy, scale=wc[:, w, i:i + 1]),
                               reads=[b_SB[sb], b_wc[w]], writes=[b_pT[r]])
                    elif i < 4 * J:
                        sc.add("dve", lambda e: e.tensor_scalar(out=pT[:, r, :], in0=SB[sb], scalar1=wc[:, w, i:i + 1], scalar2=None, op0=ALU.mult),
                               reads=[b_SB[sb], b_wc[w]], writes=[b_pT[r]])
                    else:
                        sc.add("dve", lambda e: e.scalar_tensor_tensor(out=pT[:, r, :], in0=SB[sb], scalar=wc[:, w, i:i + 1], in1=m01[:, i - 4 * J, :],
                                                                        op0=ALU.mult, op1=ALU.mult),
                               reads=[b_SB[sb], b_wc[w], b_m01], writes=[b_pT[r]])

                    def pv(e):
                        e.matmul(pac[:, 2, :], lhsT=Vh[:, i, 0:128], rhs=pT[:, r, :], start=(i == 0), stop=(i == last))
                        e.matmul(pac[:, 3, :], lhsT=Vh[:, i, 128:256], rhs=pT[:, r, :], start=(i == 0), stop=(i == last))
                        return e.matmul(pac[:, 4, :], lhsT=ones_bf[:], rhs=pT[:, r, :], start=(i == 0), stop=(i == last))
                    sc.add("pe", pv, reads=[b_pT[r], b_Vh0 if i < 4 else b_Vh, b_ones], writes=[b_pac[2], b_pac[3], b_pac[4]])
                    if i == 0 and tails:
                        tails.pop(0)()
                    if i == min(6, 4 * J - 1) and tailsS:
                        tailsS.pop(0)()
                    if i == min(9, 4 * J) and tailsB:
                        tailsB.pop(0)()
                    if i == last:
                        Js = slice(J * 512, (J + 1) * 512)
                        sc.add("act", lambda e: e.activation(out=rz[:], in_=pac[:, 4, :], func=AF.Abs), reads=[b_pac[4]], writes=[b_rz])
                        for c in range(2):
                            sc.add("dve", (lambda c=c: lambda e: e.tensor_copy(out=hb[:, c, :], in_=pac[:, 2 + c, :]))(), reads=[b_pac[2 + c]], writes=[b_hb])

                        def tail(J=J, Js=Js, mcol=mcol, hp=hp):
                            sc.add("act", lambda e: e.activation(out=bb[:], in_=Cbc[:, 0, Js], func=AF.Exp, bias=negMbc[:, mcol:mcol + 1], scale=-1.0),
                                   reads=[b_Cbc, b_negM], writes=[b_bb])
                            sc.add("dve", lambda e: e.tensor_tensor(out=rz[:], in0=rz[:], in1=bb[:], op=ALU.max), reads=[b_bb], writes=[b_rz])
                            sc.add("dve", lambda e: e.reciprocal(out=rsc2[:], in_=rz[:]), writes=[b_rz])
                            for c in range(2):
                                sc.add("dve", (lambda c=c: lambda e: e.tensor_tensor(out=hb[:, c, :], in0=hb[:, c, :], in1=rsc2[:], op=ALU.mult))(),
                                       reads=[b_rz], writes=[b_hb])
                            tailsS.append(lambda: sc.add("act", lambda e: e.activation(out=sqh[:].rearrange("p c t -> p (c t)"), in_=hb[:].rearrange("p c t -> p (c t)"), func=AF.Square),
                                                         reads=[b_hb], writes=[b_sqh]))
                            tailsB.append(lambda: tailB())

                        def tailB(J=J, Js=Js, mcol=mcol, hp=hp):
                            def ssq(e):
                                e.matmul(pac[:, 5, :], lhsT=ones_bf[:], rhs=sqh[:, 0, :], start=True, stop=False)
                                return e.matmul(pac[:, 5, :], lhsT=ones_bf[:], rhs=sqh[:, 1, :], start=False, stop=True)
                            sc.add("pe", ssq, reads=[b_sqh, b_ones], writes=[b_pac[5]])
                            sc.add("act", lambda e: e.activation(out=rn[:], in_=pac[:, 5, :], func=AF.Sqrt, bias=epsb[:], scale=1.0 / 256),
                                   reads=[b_pac[5], b_eps], writes=[b_rn])
                            sc.add("dve", lambda e: e.reciprocal(out=rn2[:], in_=rn[:]), writes=[b_rn])
                            for c in range(2):
                                a = aoc[0] % 2
                                aoc[0] += 1
                                oc = 2 * hp + c
                                sc.add("dve", (lambda c=c, oc=oc: lambda e: e.scalar_tensor_tensor(out=hb[:, c, :], in0=hb[:, c, :], scalar=onws[:, oc:oc + 1], in1=rn2[:],
                                                                                                  op0=ALU.mult, op1=ALU.mult))(),
                                       reads=[b_rn, b_onw], writes=[b_hb])
                                sc.add("dve", (lambda c=c, a=a: lambda e: e.tensor_tensor(out=ao[:, a, :], in0=hb[:, c, :], in1=moT[:, c, Js], op=ALU.mult))(),
                                       reads=[b_hb, b_moT], writes=[b_ao[a]])
                                sc.add("sp", (lambda a=a, oc=oc: lambda e: e.dma_start(out=bounce[4 + oc][:, Js], in_=ao[:, a, :]))(),
                                       reads=[b_ao[a]], writes=[b_bounce[4 + oc]], dma=True)
                        tails.append(tail)
                        if J + 1 < nJ:
                            emit_wc(J + 1)
                emit_wc(0)
                pendq = []
                for (J, i) in tiles:
                    pendq.append((J, i) + issue_S2(J, i))
                    if len(pendq) > LA:
                        rest2(*pendq.pop(0))
                while pendq:
                    rest2(*pendq.pop(0))
                while tails or tailsS or tailsB:
                    if tails:
                        tails.pop(0)()
                    if tailsS:
                        tailsS.pop(0)()
                    if tailsB:
                        tailsB.pop(0)()
                exchange(4 + 2 * hp)
                exchange(5 + 2 * hp)

            la_ = list(heads_a)
            for h_ in la_[:len(la_) // 2]:
                do_attn(h_)
            for hp_ in heads_m:
                do_mlstm(hp_)
            for h_ in la_[len(la_) // 2:]:
                do_attn(h_)
            run_block()


def _phase4(nc, sc, es, run_block, debug, A):
    import contextlib
    pac = A["pac"]; b_pac = A["b_pac"]; ptr = A["ptr"]; b_ptr = A["b_ptr"]
    ident_bf = A["ident_bf"]; b_ident = A["b_ident"]
    xo = A["xo"]; po = A["po"]; y = A["y"]
    ntk = 4 if debug is None else debug.get("ntk", 4)
    use_gather = debug is None or debug.get("exchange", True)
    ph = contextlib.ExitStack()
    with ph:
        T = lambda name, shape, dt: ph.enter_context(nc.sbuf_tensor(name, shape, dt))
        xs = T("xs", [128, 4, D], F32)
        hT = T("hT4", [128, NKC, 512], BF16)
        U = T("U", [128, 32768], BF16)
        concatT = T("catT", [128, 16, 512], BF16)
        actT = U[:, 0:22528].rearrange("p (j t) -> p j t", j=44)
        gt = U[:, 0:16384].bitcast(F32).rearrange("p (s d) -> p s d", s=4)
        et = U[:, 16384:32768].bitcast(F32).rearrange("p (s d) -> p s d", s=4)
        WP = T("WP", [128, 3, 8192], BF16)
        xn = T("xn4", [128, 2, D], BF16)
        stat = T("stat4", [128, 8], F32)
        gffn = T("gffn", [128, NKC], F32)
        gple = T("gple", [128, NKC], F32)
        wpb = T("wpb", [128, 1, D], F32)
        ppT = T("ppT", [128, 2, 512], BF16)
        pt = T("pt", [128, 2, 256], F32)
        ptb = T("ptb", [128, 2, 256], BF16)
        sg = T("sg", [128, 2, 512], F32)
        sqj = sg[:].rearrange("p a b -> p (a b)").bitcast(BF16)
        wppT = T("wppT", [128, 2, 2, 512], BF16)
        epsb = T("epsb4", [128, 1], F32)
        b_xs = [Buf() for _ in range(4)]
        b_hT = Buf(); b_cat = Buf(); b_fU = Buf()
        b_actT = [Buf() for _ in range(44)]
        b_gt = [Buf() for _ in range(4)]; b_et = [Buf() for _ in range(4)]
        b_wp = [Buf() for _ in range(3)]
        b_xn = [Buf(), Buf()]; b_stat = [Buf(), Buf()]; b_small = Buf()
        b_ppT = Buf(); b_pt = [Buf(), Buf()]; b_ptb = [Buf(), Buf()]; b_sg = [Buf(), Buf()]; b_wpp = [Buf(), Buf()]
        sc.add("sp", lambda e: e.dma_start(out=gffn[:], in_=A["wnffn"]), writes=[b_small], dma=True)
        sc.add("sp", lambda e: e.dma_start(out=gple[:], in_=A["wnple"]), writes=[b_small], dma=True)
        sc.add("sp", lambda e: e.dma_start(out=wpb[:], in_=A["wpost"].partition_broadcast(128)), writes=[b_small], dma=True)
        WBv = {k: v.rearrange("(c p) n -> p c n", p=128) for k, v in A["wb"].items()}
        woutv, wgv, wuv, wdv, wpgv, wppv = (WBv[k] for k in ("wout", "wg", "wu", "wd", "wpg", "wpp"))
        b_wsrc = {id(WBv[k]): A["b_wb"][k] for k in WBv}
        conv_done = Buf()
        sc.add("pool", lambda e: e.memset(epsb[:], EPS), writes=[conv_done, b_small] + list(A["b_wb"].values()))
        slotc = [0]
        pacn = [0]
        cnt = [0]

        def next_pac():
            i = pacn[0] % 6
            pacn[0] += 1
            return i

        def load_panel(src, nck):
            sl = slotc[0] % 3
            slotc[0] += 1
            view = WP[:, sl, 0:nck * 512].rearrange("p (c n) -> p c n", c=nck)
            sc.add("sp", lambda e: e.dma_start(out=view, in_=src), reads=[conv_done], writes=[b_wp[sl]], dma=True)
            return view, b_wp[sl]

        def norm_transpose(gain):
            base = cnt[0]
            cnt[0] += 4

            def C(s_):
                r = (base + s_) % 2
                sc.add("act", lambda e: e.activation(out=sqj[:], in_=xs[:, s_, :], func=AF.Square, accum_out=stat[:, 4 * r:4 * r + 1]),
                       reads=[b_xs[s_]], writes=[b_sg[0], b_sg[1], b_stat[r]])
                sc.add("act", lambda e: e.activation(out=stat[:, 4 * r + 1:4 * r + 2], in_=stat[:, 4 * r:4 * r + 1], func=AF.Sqrt, bias=epsb[:], scale=1.0 / D),
                       reads=[b_small], writes=[b_stat[r]])
                sc.add("dve", lambda e: e.reciprocal(out=stat[:, 4 * r + 2:4 * r + 3], in_=stat[:, 4 * r + 1:4 * r + 2]), writes=[b_stat[r]])
                sc.add("act", lambda e: e.activation(out=xn[:, r, :], in_=xs[:, s_, :], func=AF.Copy, scale=stat[:, 4 * r + 2:4 * r + 3]),
                       reads=[b_xs[s_], b_stat[r]], writes=[b_xn[r]])

            def X(s_):
                r = (base + s_) % 2
                for half in range(2):
                    def tr(e, half=half):
                        ins = None
                        for j in range(8):
                            kc = half * 8 + j
                            ins = e.transpose(out=ptr[:, half, j * 128:(j + 1) * 128], in_=xn[:, r, kc * 128:(kc + 1) * 128], identity=ident_bf[:])
                        return ins
                    sc.add("pe", tr, reads=[b_xn[r], b_ident], writes=[b_ptr[half]])
                    sc.add("dve", (lambda half=half: lambda e: e.tensor_tensor(
                        out=hT[:, half * 8:(half + 1) * 8, s_ * 128:(s_ + 1) * 128],
                        in0=ptr[:, half, :].rearrange("p (j t) -> p j t", j=8),
                        in1=gain[:, half * 8:(half + 1) * 8].unsqueeze(2).to_broadcast([128, 8, 128]), op=ALU.mult))(),
                        reads=[b_ptr[half], b_small], writes=[b_hT])
            C(0); C(1); X(0); C(2); X(1); C(3); X(2); X(3)

        def load_concat(tk):
            tsl = slice(tk * 512, (tk + 1) * 512)
            if use_gather:
                for c in (0, 1, 4, 5, 6, 7, 2, 3):
                    def ldcat(e, tsl=tsl, c=c):
                        if "rank" not in _REG:
                            _REG["rank"] = e.partition_id() % 2
                        rank = _REG["rank"]
                        gv = A["gathered_t"][c].ap().rearrange("(r p) (h n) -> p r h n", p=128, h=2)
                        return e.dma_start(out=concatT[:, c::8, :].unsqueeze(2), in_=gv[:, :, bass.ds(rank, 1), tsl])
                    sc.add("pool", ldcat, reads=[A["cc_buf"][c]], writes=[b_cat], dma=True)
            else:
                bv = A["dbg_cat"].rearrange("(c p) n -> p c n", p=128)
                sc.add("sp", (lambda tsl=tsl: lambda e: e.dma_start(out=concatT[:], in_=bv[:, :, tsl]))(), writes=[b_cat], dma=True)

        wout_pref = []
        for tk in range(ntk):
            tsl = slice(tk * 512, (tk + 1) * 512)
            for s_ in range(4):
                r0 = tk * 512 + s_ * 128
                sc.add("sp", (lambda s_=s_, r0=r0: lambda e: e.dma_start(out=xs[:, s_, :], in_=xo[r0:r0 + 128, :]))(), writes=[b_xs[s_]], dma=True)
            if tk == 0:
                load_concat(0)
            for nch in range(4):
                csl = slice(nch * 512, (nch + 1) * 512)
                if wout_pref:
                    Wp, bw = wout_pref.pop(0)
                else:
                    Wp, bw = load_panel(woutv[:, :, csl], 16)
                for s_ in range(4):
                    pi = next_pac()

                    def mm(e, s_=s_, pi=pi, Wp=Wp):
                        ins = None
                        for kc in range(NKC):
                            ins = e.matmul(pac[:, pi, :], lhsT=concatT[:, kc, s_ * 128:(s_ + 1) * 128], rhs=Wp[:, kc, :], start=(kc == 0), stop=(kc == NKC - 1))
                        return ins
                    sc.add("pe", mm, reads=[b_cat, bw], writes=[b_pac[pi]])
                    sc.add("dve", (lambda s_=s_, pi=pi, csl=csl: lambda e: e.tensor_tensor(out=xs[:, s_, csl], in0=pac[:, pi, :], in1=xs[:, s_, csl], op=ALU.add))(),
                           reads=[b_pac[pi]], writes=[b_xs[s_]])
            if tk + 1 < ntk:
                load_concat(tk + 1)
            norm_transpose(gffn)
            for pn in range(11):
                csl = slice(pn * 512, (pn + 1) * 512)
                Wg, bwg = load_panel(wgv[:, :, csl], 16)
                Wu, bwu = load_panel(wuv[:, :, csl], 16)
                for jj in range(4):
                    j = pn * 4 + jj
                    pa = next_pac()
                    pb = next_pac()
                    r = cnt[0] % 2
                    cnt[0] += 1

                    def mmg(e, W=Wg, jj=jj, pi=pa):
                        ins = None
                        for kc in range(NKC):
                            ins = e.matmul(pac[:, pi, :], lhsT=W[:, kc, jj * 128:(jj + 1) * 128], rhs=hT[:, kc, :], start=(kc == 0), stop=(kc == NKC - 1))
                        return ins

                    def mmu(e, W=Wu, jj=jj, pi=pb):
                        ins = None
                        for kc in range(NKC):
                            ins = e.matmul(pac[:, pi, :], lhsT=W[:, kc, jj * 128:(jj + 1) * 128], rhs=hT[:, kc, :], start=(kc == 0), stop=(kc == NKC - 1))
                        return ins
                    sc.add("pe", mmg, reads=[b_hT, bwg], writes=[b_pac[pa]])
                    sc.add("pe", mmu, reads=[b_hT, bwu], writes=[b_pac[pb]])
                    sc.add("act", (lambda r=r, pa=pa: lambda e: e.activation(out=sg[:, r, :], in_=pac[:, pa, :], func=AF.Silu))(),
                           reads=[b_pac[pa]], writes=[b_sg[r]])
                    wl = [b_actT[j], (b_gt[j // 8] if j < 32 else b_et[(j - 32) // 8])]
                    sc.add("dve", (lambda r=r, pb=pb, j=j: lambda e: e.tensor_tensor(out=actT[:, j, :], in0=pac[:, pb, :], in1=sg[:, r, :], op=ALU.mult))(),
                           reads=[b_pac[pb], b_sg[r]], writes=wl)
            for nch in range(4):
                csl = slice(nch * 512, (nch + 1) * 512)
                for (c0, c1) in ((0, 16), (16, 32), (32, 44)):
                    Wd, bwd = load_panel(wdv[:, c0:c1, csl], c1 - c0)
                    for s_ in range(4):
                        def mmd(e, s_=s_, c0=c0, c1=c1, Wd=Wd):
                            ins = None
                            for j in range(c0, c1):
                                ins = e.matmul(pac[:, s_, :], lhsT=actT[:, j, s_ * 128:(s_ + 1) * 128], rhs=Wd[:, j - c0, :], start=(j == 0), stop=(j == 43))
                            return ins
                        sc.add("pe", mmd, reads=[bwd, b_fU] + b_actT[c0:c1], writes=[b_pac[s_]])
                for s_ in range(4):
                    sc.add("dve", (lambda s_=s_, csl=csl: lambda e: e.tensor_tensor(out=xs[:, s_, csl], in0=pac[:, s_, :], in1=xs[:, s_, csl], op=ALU.add))(),
                           reads=[b_pac[s_]], writes=[b_xs[s_]])
            pacn[0] = 4
            norm_transpose(gple)
            for s_ in range(4):
                r = cnt[0] % 2
                cnt[0] += 1
                r0 = tk * 512 + s_ * 128
                sc.add("sp", (lambda r=r, r0=r0: lambda e: e.dma_start(out=pt[:, r, :], in_=po[r0:r0 + 128, :]))(), writes=[b_pt[r]], dma=True)
                sc.add("dve", (lambda r=r: lambda e: e.tensor_copy(out=ptb[:, r, :], in_=pt[:, r, :]))(), reads=[b_pt[r]], writes=[b_ptb[r]])

                def trp(e, r=r):
                    e.transpose(out=ptr[:, 0, 0:128], in_=ptb[:, r, 0:128], identity=ident_bf[:])
                    return e.transpose(out=ptr[:, 0, 128:256], in_=ptb[:, r, 128:256], identity=ident_bf[:])
                sc.add("pe", trp, reads=[b_ptb[r], b_ident], writes=[b_ptr[0]])
                sc.add("dve", (lambda s_=s_: lambda e: e.tensor_copy(out=ppT[:, :, s_ * 128:(s_ + 1) * 128], in_=ptr[:, 0, 0:256].rearrange("p (c t) -> p c t", c=2)))(),
                       reads=[b_ptr[0]], writes=[b_ppT])
            for nch in range(4):
                csl = slice(nch * 512, (nch + 1) * 512)
                Wp, bw = load_panel(wpgv[:, :, csl], 16)
                rw = nch % 2
                sc.add("sp", (lambda rw=rw, csl=csl: lambda e: e.dma_start(out=wppT[:, rw], in_=wppv[:, :, csl]))(), reads=[conv_done], writes=[b_wpp[rw]], dma=True)
                for s_ in range(4):
                    pa = next_pac()
                    pb = next_pac()

                    def mmq(e, s_=s_, pi=pa, Wp=Wp):
                        ins = None
                        for kc in range(NKC):
                            ins = e.matmul(pac[:, pi, :], lhsT=hT[:, kc, s_ * 128:(s_ + 1) * 128], rhs=Wp[:, kc, :], start=(kc == 0), stop=(kc == NKC - 1))
                        return ins

                    def mme(e, s_=s_, pi=pb, rw=rw):
                        e.matmul(pac[:, pi, :], lhsT=ppT[:, 0, s_ * 128:(s_ + 1) * 128], rhs=wppT[:, rw, 0, :], start=True, stop=False)
                        return e.matmul(pac[:, pi, :], lhsT=ppT[:, 1, s_ * 128:(s_ + 1) * 128], rhs=wppT[:, rw, 1, :], start=False, stop=True)
                    sc.add("pe", mmq, reads=[b_hT, bw], writes=[b_pac[pa]])
                    sc.add("pe", mme, reads=[b_ppT, b_wpp[rw]], writes=[b_pac[pb]])
                    sc.add("act", (lambda s_=s_, pa=pa, csl=csl: lambda e: e.activation(out=gt[:, s_, csl], in_=pac[:, pa, :], func=AF.Sigmoid))(),
                           reads=[b_pac[pa]], writes=[b_gt[s_], b_fU])
                    sc.add("dve", (lambda s_=s_, pb=pb, csl=csl: lambda e: e.tensor_copy(out=et[:, s_, csl], in_=pac[:, pb, :]))(),
                           reads=[b_pac[pb]], writes=[b_et[s_], b_fU])
            if tk + 1 < ntk:
                for nch in range(2):
                    wout_pref.append(load_panel(woutv[:, :, nch * 512:(nch + 1) * 512], 16))
            for s_ in range(4):
                r = cnt[0] % 2
                cnt[0] += 1
                r0 = tk * 512 + s_ * 128
                sc.add("act", (lambda r=r, s_=s_: lambda e: e.activation(out=sqj[:], in_=et[:, s_, :], func=AF.Square, accum_out=stat[:, 4 * r:4 * r + 1]))(),
                       reads=[b_et[s_]], writes=[b_sg[0], b_sg[1], b_stat[r]])
                sc.add("act", (lambda r=r: lambda e: e.activation(out=stat[:, 4 * r + 1:4 * r + 2], in_=stat[:, 4 * r:4 * r + 1], func=AF.Sqrt, bias=epsb[:], scale=1.0 / D))(),
                       reads=[b_small], writes=[b_stat[r]])
                sc.add("dve", (lambda r=r: lambda e: e.reciprocal(out=stat[:, 4 * r + 2:4 * r + 3], in_=stat[:, 4 * r + 1:4 * r + 2]))(), writes=[b_stat[r]])
                sc.add("dve", (lambda r=r, s_=s_: lambda e: e.scalar_tensor_tensor(out=et[:, s_, :], in0=et[:, s_, :], scalar=stat[:, 4 * r + 2:4 * r + 3], in1=wpb[:, 0, :],
                                                                                  op0=ALU.mult, op1=ALU.mult))(),
                       reads=[b_stat[r], b_small], writes=[b_et[s_]])
                sc.add("dve", (lambda s_=s_: lambda e: e.tensor_tensor(out=et[:, s_, :], in0=et[:, s_, :], in1=gt[:, s_, :], op=ALU.mult))(),
                       reads=[b_gt[s_]], writes=[b_et[s_]])
                sc.add("dve", (lambda s_=s_: lambda e: e.tensor_tensor(out=xs[:, s_, :], in0=xs[:, s_, :], in1=et[:, s_, :], op=ALU.add))(),
                       reads=[b_et[s_]], writes=[b_xs[s_]])
                sc.add("sp", (lambda s_=s_, r0=r0: lambda e: e.dma_start(out=y[r0:r0 + 128, :], in_=xs[:, s_, :]))(), reads=[b_xs[s_]], dma=True)
        run_block()


def _act_evac8(e, hT, ptr, gmix, half, pb, s_):
    ins = None
    for j in range(8):
        kc = half * 8 + j
        ins = e.activation(out=hT[:, kc, s_ * 128:(s_ + 1) * 128], in_=ptr[:, pb, j * 128:(j + 1) * 128],
                           func=AF.Copy, scale=gmix[:, kc:kc + 1])
    return ins


def _finish(nc, sc, es, y, xo, run_block):
    t = es.enter_context(nc.sbuf_tensor("dbg_t", [128, 2048], F32))
    b = Buf()
    for i in range(y.shape[0] // 128):
        sc.add("sp", (lambda i=i: lambda e: e.dma_start(out=t[:], in_=xo[i * 128:(i + 1) * 128, :]))(), writes=[b], dma=True)
        sc.add("sp", (lambda i=i: lambda e: e.dma_start(out=y[i * 128:(i + 1) * 128, :], in_=t[:]))(), reads=[b], dma=True)
    run_block()


def prep_inputs(inp):
    f = lambda a: np.ascontiguousarray(np.asarray(a, dtype=np.float32))
    x = f(inp["x"]); p = f(inp["p"])[0]
    w_in = f(inp["w_in"])[0]
    t16 = lambda v: np.ascontiguousarray(f(v)[0].reshape(16, 128).T)
    perm = np.concatenate([np.arange(0, 512), np.arange(1024, 1536), np.arange(512, 1024), np.arange(1536, 2048)])
    shared = {
        "wnmix": t16(inp["w_norm_mix"]),
        "qkg": np.ascontiguousarray(np.stack([f(inp["q_norm_w"])[0], f(inp["k_norm_w"])[0]], axis=1)),
        "wout": np.ascontiguousarray(f(inp["w_out"])[0][perm]),
        "wnffn": t16(inp["w_norm_ffn"]),
        "wg": f(inp["w_ffn_gate"])[0], "wu": f(inp["w_ffn_up"])[0], "wd": f(inp["w_ffn_down"])[0],
        "wnple": t16(inp["w_norm_ple"]),
        "wpg": f(inp["w_ple_gate"])[0], "wpp": f(inp["w_ple_proj"])[0],
        "wpost": f(inp["w_ple_post_norm"])[0].reshape(1, 2048),
    }
    cw = f(inp["mlstm_conv_w"])[0]
    cb = f(inp["mlstm_conv_b"])[0]
    maps = []
    for c in range(8):
        b, g = divmod(c, 2)
        cols = np.concatenate([
            np.arange(512 * g, 512 * g + 512),
            1024 + np.arange(512 * g, 512 * g + 512),
            3080 + np.arange(256 * g, 256 * g + 256),
            3592 + np.arange(256 * g, 256 * g + 256),
            5136 + np.arange(512 * g, 512 * g + 512),
            2048 + np.arange(512 * g, 512 * g + 512),
            4104 + np.arange(512 * g, 512 * g + 512),
            3072 + np.arange(4 * g, 4 * g + 4),
            5128 + np.arange(2 * g, 2 * g + 2),
            5132 + np.arange(2 * g, 2 * g + 2),
        ])
        ch = np.concatenate([np.arange(256 * g, 256 * g + 256), 512 + np.arange(256 * g, 256 * g + 256)])
        m = dict(shared)
        m["xb"] = x[b]
        m["xo"] = np.ascontiguousarray(x[b, 2048 * g:2048 * g + 2048])
        m["po"] = np.ascontiguousarray(p[b, 2048 * g:2048 * g + 2048])
        m["w1"] = np.ascontiguousarray(w_in[:, cols])
        m["bias8"] = np.concatenate([f(inp["fox_f_bias"])[0, 4 * g:4 * g + 4], f(inp["mlstm_i_bias"])[0, 2 * g:2 * g + 2],
                                     f(inp["mlstm_f_bias"])[0, 2 * g:2 * g + 2]]).reshape(8, 1).astype(np.float32)
        m["convw"] = np.ascontiguousarray(cw[:, ch].reshape(4, 4, 128).transpose(2, 1, 0))
        m["convb"] = np.ascontiguousarray(cb[ch].reshape(4, 128).T)
        m["onw"] = np.ascontiguousarray(f(inp["mlstm_out_norm_w"])[0, 512 * g:512 * g + 512].reshape(4, 128).T)
        maps.append(m)
    return maps


_NC_CACHE = {}


def kernel(**inputs):
    maps = prep_inputs(inputs)
    if "nc" not in _NC_CACHE:
        _NC_CACHE["nc"] = build_nc()
    nc = _NC_CACHE["nc"]
    res = run_bass_kernel_spmd(nc, maps, core_ids=list(range(8)))
    out = np.empty((4, S, D), dtype=np.float32)
    for c in range(8):
        b, g = divmod(c, 2)
        out[b, 2048 * g:2048 * g + 2048] = np.asarray(res.results[c]["y"], dtype=np.float32)
    return out
```
